# Optimizing a Trainium2 kernel written in Bass

```python
import jax, jax.numpy as jnp
from jax import lax
import numpy as np

D_MODEL = 1024
BATCH = 32
SEQ = 2048
DEPTH = 1

CHUNK = 64
Q_BLOCK = 128
EPS = 1e-6
CONV_WIDTH = D_MODEL
CONV_KERNEL = 31
N_HEADS = 16
HEAD_DIM = 64
V_DIM = 64
Q_LORA = 256
KV_LORA = 128
IDX_HEADS = 8
IDX_DIM = 64
IDX_TOPK_MAX = 256
D_FF = 4 * D_MODEL
N_BRANCH = 2
IN_SIZES = (2 * CONV_WIDTH, Q_LORA, KV_LORA, IDX_DIM, IDX_HEADS, N_BRANCH * D_MODEL)
D_IN = sum(IN_SIZES)

kernel_name = "hybrid_conformer_dsa_block"


def rms_norm(x, g):
    xf = x.astype(jnp.float32)
    y = xf * lax.rsqrt(jnp.mean(xf * xf, axis=-1, keepdims=True) + EPS)
    return (y * g.astype(jnp.float32)).astype(x.dtype)


def layer_norm(x, g, b):
    xf = x.astype(jnp.float32)
    mu = jnp.mean(xf, axis=-1, keepdims=True)
    var = jnp.mean(jnp.square(xf - mu), axis=-1, keepdims=True)
    y = (xf - mu) * lax.rsqrt(var + EPS)
    return (y * g.astype(jnp.float32) + b.astype(jnp.float32)).astype(x.dtype)


def split_columns(u):
    offs = np.cumsum(IN_SIZES)[:-1].tolist()
    return jnp.split(u, offs, axis=-1)


def conv_branch(u_conv, dw_w, dw_b, ln_g, ln_b, w_pw):
    a, gate = jnp.split(u_conv, 2, axis=-1)
    v = a * jax.nn.sigmoid(gate)
    y = lax.conv_general_dilated(
        v, dw_w[:, None, :].astype(v.dtype), window_strides=(1,),
        padding=[(CONV_KERNEL - 1, 0)],
        dimension_numbers=("NWC", "WIO", "NWC"),
        feature_group_count=CONV_WIDTH)
    y = layer_norm(y + dw_b, ln_g, ln_b)
    return jax.nn.silu(y) @ w_pw


def sparse_attention_branch(c_q, c_kv, k_idx_raw, w_idx_raw, q_norm_g, kv_norm_g,
                            w_uq, w_uk, w_uv, w_qi, kidx_ln_g, kidx_ln_b, w_attn_out):
    B, L, _ = c_q.shape
    c_q = rms_norm(c_q, q_norm_g)
    c_kv = rms_norm(c_kv, kv_norm_g)
    q = (c_q @ w_uq).reshape(B, L, N_HEADS, HEAD_DIM)
    q_lat = jnp.einsum("blhd,hdc->blhc", q, w_uk)
    q_idx = (c_q @ w_qi).reshape(B, L, IDX_HEADS, IDX_DIM).astype(jnp.float32)
    k_idx = layer_norm(k_idx_raw, kidx_ln_g, kidx_ln_b).astype(jnp.float32)
    w_idx = w_idx_raw.astype(jnp.float32) * (IDX_HEADS ** -0.5 * IDX_DIM ** -0.5)
    top_k = min(IDX_TOPK_MAX, L // 4)
    key_chunk = jnp.arange(L) // CHUNK

    def block(i):
        start = i * Q_BLOCK
        qi = lax.dynamic_slice_in_dim(q_idx, start, Q_BLOCK, axis=1)
        wi = lax.dynamic_slice_in_dim(w_idx, start, Q_BLOCK, axis=1)
        ql = lax.dynamic_slice_in_dim(q_lat, start, Q_BLOCK, axis=1)
        q_chunk = (start + jnp.arange(Q_BLOCK)) // CHUNK
        admissible = key_chunk[None, :] <= q_chunk[:, None]
        logits = jnp.einsum("bqhd,bsd->bqhs", qi, k_idx)
        score = jnp.einsum("bqh,bqhs->bqs", wi, jax.nn.relu(logits))
        score = jnp.where(admissible[None], score, -jnp.inf)
        _, sel = lax.top_k(score, top_k)
        valid = key_chunk[sel] <= q_chunk[None, :, None]
        c_sel = jax.vmap(lambda c, s: c[s])(c_kv, sel)
        s = jnp.einsum("bqhc,bqkc->bqhk", ql, c_sel).astype(jnp.float32) * (HEAD_DIM ** -0.5)
        s = jnp.where(valid[:, :, None, :], s, -jnp.inf)
        p = jax.nn.softmax(s, axis=-1).astype(c_sel.dtype)
        o_lat = jnp.einsum("bqhk,bqkc->bqhc", p, c_sel)
        o = jnp.einsum("bqhc,hcv->bqhv", o_lat, w_uv).reshape(B, Q_BLOCK, N_HEADS * V_DIM)
        return o @ w_attn_out

    out = lax.map(block, jnp.arange(L // Q_BLOCK))
    return jnp.moveaxis(out, 0, 1).reshape(B, L, D_MODEL)


def setup_inputs(seed: int = 0) -> dict:
    key = jax.random.key(seed)
    ks = jax.random.split(key, 24)
    f = jnp.float32

    def nrm(k, shape, scale):
        return jax.random.normal(k, shape, f) * scale

    def gain(k, shape):
        return 1.0 + 0.02 * jax.random.normal(k, shape, f)

    Ld = DEPTH
    return {
        "x": jax.random.normal(ks[0], (BATCH, SEQ, D_MODEL), f),
        "attn_norm_g": gain(ks[1], (Ld, D_MODEL)),
        "w_in": nrm(ks[2], (Ld, D_MODEL, D_IN), D_MODEL ** -0.5),
        "b_gate": nrm(ks[3], (Ld, N_BRANCH * D_MODEL), 0.1),
        "dw_w": nrm(ks[4], (Ld, CONV_KERNEL, CONV_WIDTH), CONV_KERNEL ** -0.5),
        "dw_b": nrm(ks[5], (Ld, CONV_WIDTH), 0.02),
        "conv_ln_g": gain(ks[6], (Ld, CONV_WIDTH)),
        "conv_ln_b": nrm(ks[7], (Ld, CONV_WIDTH), 0.02),
        "w_conv_out": nrm(ks[8], (Ld, CONV_WIDTH, D_MODEL), CONV_WIDTH ** -0.5),
        "q_norm_g": gain(ks[9], (Ld, Q_LORA)),
        "kv_norm_g": gain(ks[10], (Ld, KV_LORA)),
        "w_uq": nrm(ks[11], (Ld, Q_LORA, N_HEADS * HEAD_DIM), Q_LORA ** -0.5),
        "w_uk": nrm(ks[12], (Ld, N_HEADS, HEAD_DIM, KV_LORA), KV_LORA ** -0.5),
        "w_uv": nrm(ks[13], (Ld, N_HEADS, KV_LORA, V_DIM), KV_LORA ** -0.5),
        "w_qi": nrm(ks[14], (Ld, Q_LORA, IDX_HEADS * IDX_DIM), Q_LORA ** -0.5),
        "kidx_ln_g": gain(ks[15], (Ld, IDX_DIM)),
        "kidx_ln_b": nrm(ks[16], (Ld, IDX_DIM), 0.02),
        "w_attn_out": nrm(ks[17], (Ld, N_HEADS * V_DIM, D_MODEL), (N_HEADS * V_DIM) ** -0.5),
        "w_o": nrm(ks[18], (Ld, D_MODEL, D_MODEL), D_MODEL ** -0.5),
        "mlp_norm_g": gain(ks[19], (Ld, D_MODEL)),
        "w_ff1": nrm(ks[20], (Ld, D_MODEL, D_FF), D_MODEL ** -0.5),
        "w_ff2": nrm(ks[21], (Ld, D_FF, D_MODEL), D_FF ** -0.5),
        "final_norm_g": gain(ks[22], (D_MODEL,)),
    }


def reference(x, attn_norm_g, w_in, b_gate, dw_w, dw_b, conv_ln_g, conv_ln_b, w_conv_out,
              q_norm_g, kv_norm_g, w_uq, w_uk, w_uv, w_qi, kidx_ln_g, kidx_ln_b, w_attn_out,
              w_o, mlp_norm_g, w_ff1, w_ff2, final_norm_g):
    B, L, _ = x.shape
    for i in range(DEPTH):
        h = rms_norm(x, attn_norm_g[i])
        u = h @ w_in[i]
        u_conv, c_q, c_kv, k_idx_raw, w_idx_raw, u_gate = split_columns(u)
        y_a = conv_branch(u_conv, dw_w[i], dw_b[i], conv_ln_g[i], conv_ln_b[i], w_conv_out[i])
        y_b = sparse_attention_branch(c_q, c_kv, k_idx_raw, w_idx_raw, q_norm_g[i], kv_norm_g[i],
                                      w_uq[i], w_uk[i], w_uv[i], w_qi[i], kidx_ln_g[i],
                                      kidx_ln_b[i], w_attn_out[i])
        gates = jax.nn.sigmoid(u_gate + b_gate[i]).reshape(B, L, N_BRANCH, D_MODEL)
        merged = gates[:, :, 0, :] * y_a + gates[:, :, 1, :] * y_b
        x = x + merged @ w_o[i]
        hm = rms_norm(x, mlp_norm_g[i])
        x = x + jnp.square(jax.nn.relu(hm @ w_ff1[i])) @ w_ff2[i]
    return rms_norm(x, final_norm_g)
```

```python
import numpy as np
import os
DBG = os.environ.get('KDBG', '')
from contextlib import ExitStack
import concourse.bass as bass
import concourse.mybir as mybir
from concourse.bass_utils import run_bass_kernel_spmd

F32 = mybir.dt.float32
BF16 = mybir.dt.bfloat16
ALU = mybir.AluOpType
AF = mybir.ActivationFunctionType
AX = mybir.AxisListType

D = 1024
DIN = 4552
DFF = 4096
EPS = 1e-6
NIT = 16
NSLOT = 3
OAC_B = 2
DEN_B = 4
ACT_FRAC = 0.42
ACT_FRAC_B0 = 0.2
NDUM_A = 0
NDUM_B = 0
ACT_COUNT = True
NEG = -1.0e30

C_GATTN, C_BGATE, C_DWB, C_LNG, C_LNB, C_QG, C_KIG, C_KIB, C_GMLP, C_DWW = 0, 8, 24, 32, 40, 48, 50, 51, 52, 60
NCOLS = 60 + 8 * 31


class Prog:
    EPOCH = 20000

    def __init__(self, nc):
        self.nc = nc
        self.E = {'pe': nc.tensor, 'act': nc.scalar, 'dve': nc.vector, 'pool': nc.gpsimd, 'sp': nc.sync}
        self.cnt = {e: 0 for e in self.E}
        self.sems = {e: [] for e in self.E}
        self.seen = {e: {} for e in self.E}
        self.last_w = {}
        self.readers = {}
        self.ndma_sems = 8
        self.dma_sems = [nc.alloc_semaphore(name=f"dma{i}") for i in range(self.ndma_sems)]
        self.dma_cnt = [0] * self.ndma_sems
        self.dma_rr = 0
        self.bar = {e: None for e in self.E}
        self.dead = False

    def _sem(self, e, idx):
        ep = (idx - 1) // self.EPOCH
        while len(self.sems[e]) <= ep:
            self.sems[e].append(self.nc.alloc_semaphore(name=f"s_{e}_{len(self.sems[e])}"))
        return self.sems[e][ep], (idx - 1) % self.EPOCH + 1

    def _wait(self, e, dep):
        f, idx = dep
        if self.seen[e].get(f, 0) >= idx:
            return
        self.seen[e][f] = idx
        if isinstance(f, tuple):
            self.E[e].wait_ge(self.dma_sems[f[1]], idx)
        else:
            s, v = self._sem(f, idx)
            self.E[e].wait_ge(s, v)

    def _deps(self, e, reads, writes):
        deps = []
        for k in reads:
            w = self.last_w.get(k)
            if w is not None:
                if w[0] == e and e == 'pe':
                    continue
                deps.append(w)
        for k in writes:
            w = self.last_w.get(k)
            if w is not None and not (w[0] == e and e == 'pe'):
                deps.append(w)
            for r in self.readers.get(k, ()):
                if not (r[0] == e and e == 'pe'):
                    deps.append(r)
        b = self.bar[e]
        if b is not None:
            deps.extend(d for d in b if d[0] != e)
            self.bar[e] = None
        return deps

    def _record(self, me, reads, writes):
        for k in reads:
            lst = self.readers.setdefault(k, [])
            lst[:] = [r for r in lst if r[0] != me[0]]
            lst.append(me)
        for k in writes:
            self.last_w[k] = me
            self.readers[k] = []

    def op(self, e, fn, reads=(), writes=()):
        if self.dead:
            return None
        for d in self._deps(e, reads, writes):
            self._wait(e, d)
        ins = fn(self.E[e])
        self.cnt[e] += 1
        idx = self.cnt[e]
        s, v = self._sem(e, idx)
        ins.then_inc(s, 1)
        self._record((e, idx), reads, writes)
        return ins

    def group(self, e, fns, reads=(), writes=()):
        if self.dead:
            return None
        for d in self._deps(e, reads, writes):
            self._wait(e, d)
        ins = None
        for fn in fns:
            ins = fn(self.E[e])
        self.cnt[e] += 1
        idx = self.cnt[e]
        s, v = self._sem(e, idx)
        ins.then_inc(s, 1)
        self._record((e, idx), reads, writes)

    def dma(self, q, out, in_, reads=(), writes=()):
        if self.dead:
            return None
        si = self.dma_rr
        self.dma_rr = (self.dma_rr + 1) % self.ndma_sems
        for d in self._deps(q, reads, writes):
            self._wait(q, d)
        if self.dma_cnt[si] > 0:
            self._wait(q, (('dma', si), self.dma_cnt[si]))
        self.E[q].dma_start(out=out, in_=in_).then_inc(self.dma_sems[si], 16)
        self.dma_cnt[si] += 16
        me = (('dma', si), self.dma_cnt[si])
        self._record(me, reads, writes)
        return me

    def barrier(self):
        pts = [(e, self.cnt[e]) for e in self.E if self.cnt[e] > 0]
        pts += [(('dma', i), self.dma_cnt[i]) for i in range(self.ndma_sems) if self.dma_cnt[i] > 0]
        for e in self.E:
            self.bar[e] = list(pts)

    def finish(self):
        self.barrier()
        for e in self.E:
            for d in self.bar[e]:
                if d[0] != e:
                    self._wait(e, d)
            self.bar[e] = None


class _Stop(Exception):
    pass


def build(NB, L, TOPK, limit=99):
    NT = L // 128
    NCH = L // 512
    nc = bass.Bass("TRN2", target_bir_lowering=False)
    dt_in = lambda name, shape: nc.dram_tensor(name, shape, F32, kind="ExternalInput").ap()
    x = dt_in("x", [NB, L, D])
    w_in = dt_in("w_in", [D, DIN])
    w_co = dt_in("w_co", [D, D])
    w_uq = dt_in("w_uq", [256, 1024])
    w_qi = dt_in("w_qi", [256, 512])
    w_ukp = dt_in("w_ukp", [128, 16 * 128])
    w_uvp = dt_in("w_uvp", [128, 16 * 128])
    w_ao = dt_in("w_ao", [D, D])
    w_o = dt_in("w_o", [D, D])
    w_1 = dt_in("w_1", [D, DFF])
    w_2 = dt_in("w_2", [DFF, D])
    cols_d = dt_in("cols", [128, NCOLS])
    bcs_d = dt_in("bcs", [128, 128 + 1024])
    y = nc.dram_tensor("y", [NB, L, D], F32, kind="ExternalOutput").ap()
    scr = lambda name, shape: nc.dram_tensor(name, shape, BF16, kind="Internal").ap()
    s_win = scr("s_win", [D, DIN])
    s_wco = scr("s_wco", [D, D])
    s_wao = scr("s_wao", [D, D])
    s_wo = scr("s_wo", [D, D])
    s_w1 = scr("s_w1", [D, DFF])
    s_w2 = scr("s_w2", [DFF, D])
    s_dg = scr("s_dg", [8, 128, 31 * 128])

    p = Prog(nc)
    with ExitStack() as es:
        uid = [0]

        def _alloc(stack, name, shape, dt):
            uid[0] += 1
            return stack.enter_context(nc.sbuf_tensor(f"sb{uid[0]}_{name}", shape, dt))

        S = lambda name, shape, dt: _alloc(es, name, shape, dt)
        ps = es.enter_context(nc.psum_tensor("ps", [128, 8, 512], F32))

        def psT(b):
            return ps[:, b, :].bitcast(BF16)

        cols = S("cols", [128, NCOLS], F32)
        bcs = S("bcs", [128, 1152], F32)
        ident = S("ident", [128, 128], BF16)
        ones = S("ones", [128, 128], BF16)
        pw2 = S("pw2", [128, NIT + 1], F32)
        wsm = S("wsm", [128, 8, 456], BF16)
        wuq = S("wuq", [128, 2, 1024], BF16)
        wqi = S("wqi", [128, 2, 512], BF16)
        wukp = S("wukp", [128, 16, 128], BF16)
        wuvp = S("wuvp", [128, 16, 128], BF16)
        slots = [S(f"slot{i}", [128, 4096], BF16) for i in range(NSLOT)]
        ckv_tok = S("ckv_tok", [128, NT, 128], BF16)
        ckvT = S("ckvT", [128, L], BF16)
        kidxT = S("kidxT", [64, L], BF16)
        widx = S("widx", [128, NT, 8], F32)
        xt = [S(f"xt{i}", [128, D], F32) for i in range(4)]
        hT = S("hT", [128, 8, 512], BF16)
        cqT = S("cqT", [128, 2, 512], BF16)
        mergedT = S("mergedT", [128, 8, 512], BF16)
        oT = S("oT", [128, 8, 512], BF16)
        st = S("st", [128, 64], F32)
        junkAs = [S(f"junkA{i}", [128, 1024], BF16) for i in range(1)]
        jrr = [0]

        def rsqrt(out_ap, in_ap, reads, writes):
            p.op('act', lambda e: e.activation(out=in_ap, in_=in_ap, func=AF.Sqrt), reads=reads, writes=reads)
            p.op('dve', lambda e: e.reciprocal(out=out_ap, in_=in_ap), reads=reads, writes=writes)

        def sq_accum(in_ap, n, acc_ap, reads, writes):
            jt, jk = JA()
            p.op('act', lambda e: e.activation(out=jt[:, 0:n], in_=in_ap, func=AF.Square, accum_out=acc_ap), reads=reads, writes=list(writes) + [jk])

        def JA():
            return junkAs[0], ('junkA', 0)
        vT = S("vT", [128, 8, 544], BF16)

        kvg_bc = bcs[:, 0:128]
        gfin_bc = bcs[:, 128:1152]

        psr = [0]

        def psalloc(n=1):
            b = psr[0]
            if b % n:
                b += n - b % n
            if b + n > 8:
                b = 0
            psr[0] = (b + n) % 8
            return b

        def PK(b, n=1):
            return [('ps', b + i) for i in range(n)]

        slot_rr = [0]

        def wload(dram_ap, view_fn):
            s = slot_rr[0]
            slot_rr[0] = (s + 1) % NSLOT
            v = view_fn(slots[s])
            p.dma('sp', v, dram_ap, writes=[('slot', s)])
            return v, ('slot', s)

        v_k512 = lambda t: t[:, :].rearrange("p (k c) -> p k c", k=8)
        v_dg = lambda t: t[:, 0:31 * 128].rearrange("p (k c) -> p k c", k=31)

        def wcols(s_ap, c0, cw=512):
            return s_ap[:, c0:c0 + cw].rearrange("(k p) c -> p k c", p=128)

        evac_rr = [0]

        def evac_eng():
            evac_rr[0] ^= 1
            return 'act' if evac_rr[0] else 'dve'

        def evac(out, in_, reads, writes):
            e = 'dve'
            p.op(e, copy_op(e, out, in_), reads=reads, writes=writes)

        def copy_op(e, out, in_):
            if e == 'act':
                return lambda en: en.activation(out=out, in_=in_, func=AF.Copy)
            return lambda en: en.tensor_copy(out=out, in_=in_)

        p.dma('sp', cols[:], cols_d, writes=['cols'])
        p.dma('sp', bcs[:], bcs_d, writes=['bcs'])
        p.op('dve', lambda e: e.memset(ones[:], 1.0), writes=['ones'])
        for n in range(NIT + 1):
            p.op('dve', lambda e, n=n: e.memset(pw2[:, n:n + 1], 0.5 ** (n + 1)), writes=['pw2'])
        p.op('pool', lambda e: e.affine_select(out=ident[:], in_=ones[:], pattern=[[1, 128]], compare_op=ALU.is_equal,
                                               fill=0.0, base=0, channel_multiplier=-1), reads=['ones'], writes=['ident'])
        with ExitStack() as es2:
            S2 = lambda name, shape, dt: _alloc(es2, name, shape, dt)
            st32 = [S2(f"st32_{i}", [128, 2048], F32) for i in range(4)]
            st16 = [S2(f"st16_{i}", [128, 2048], BF16) for i in range(4)]
            dgs = [S2(f"dgs{i}", [128, 31, 128], BF16) for i in range(2)]
            ci = [0]

            def conv_piece(src_ap, cw, dst_dram=None, dst_sb=None, scale=None):
                s = ci[0] % 4
                ci[0] += 1
                p.dma('sp', st32[s][:, :cw], src_ap, writes=[('st32', s)])
                if dst_sb is not None:
                    o = dst_sb
                    wr = ['wres']
                else:
                    o = st16[s][:, :cw]
                    wr = [('st16', s)]
                sc = 1.0 if scale is None else scale
                p.op('dve', lambda e: e.tensor_scalar(out=o, in0=st32[s][:, :cw], scalar1=sc, scalar2=None, op0=ALU.mult),
                     reads=[('st32', s), 'cols'], writes=wr)
                if dst_dram is not None:
                    p.dma('act', dst_dram, st16[s][:, :cw], reads=[('st16', s)], writes=['scratch'])

            def convert(W, dst, R, C, scale_off=None):
                for rt in range(R // 128):
                    sc = None if scale_off is None else cols[:, scale_off + rt:scale_off + rt + 1]
                    for c0 in range(0, C, 2048):
                        cw = min(2048, C - c0)
                        conv_piece(W[rt * 128:(rt + 1) * 128, c0:c0 + cw], cw,
                                   dst_dram=dst[rt * 128:(rt + 1) * 128, c0:c0 + cw], scale=sc)

            convert(w_in, s_win, D, DIN, C_GATTN)
            convert(w_co, s_wco, D, D)
            convert(w_ao, s_wao, D, D)
            convert(w_o, s_wo, D, D)
            convert(w_1, s_w1, D, DFF, C_GMLP)
            convert(w_2, s_w2, DFF, D)
            for k in range(8):
                conv_piece(w_in[k * 128:(k + 1) * 128, 2048:2504], 456, dst_sb=wsm[:, k, :], scale=cols[:, C_GATTN + k:C_GATTN + k + 1])
            for kk in range(2):
                conv_piece(w_uq[kk * 128:(kk + 1) * 128, :], 1024, dst_sb=wuq[:, kk, :], scale=cols[:, C_QG + kk:C_QG + kk + 1])
                conv_piece(w_qi[kk * 128:(kk + 1) * 128, :], 512, dst_sb=wqi[:, kk, :], scale=cols[:, C_QG + kk:C_QG + kk + 1])
            conv_piece(w_ukp, 2048, dst_sb=wukp[:, :, :].rearrange("p a b -> p (a b)"))
            conv_piece(w_uvp, 2048, dst_sb=wuvp[:, :, :].rearrange("p a b -> p (a b)"))
            for ct in range(8):
                s = ct % 2
                fns = []
                for k in range(31):
                    cidx = C_DWW + ct * 31 + k
                    fns.append(lambda e, k=k, cidx=cidx, s=s: e.tensor_scalar(out=dgs[s][:, k, :], in0=ident[:], scalar1=cols[:, cidx:cidx + 1],
                                                                              scalar2=None, op0=ALU.mult))
                p.group('dve', fns, reads=['ident', 'cols'], writes=[('dgs', s)])
                p.dma('act', s_dg[ct], dgs[s][:, :, :].rearrange("p a b -> p (a b)"), reads=[('dgs', s)], writes=['scratch'])
            p.barrier()
        p.barrier()

        try:
          if limit < 1:
              p.dead = True
          for b in range(NB):
              for c in range(NCH):
                  t0 = c * 512
                  with ExitStack() as esa:
                      SA = lambda name, shape, dt: _alloc(esa, name, shape, dt)
                      qT = SA("qT", [128, 8, 128], BF16)
                      qlatT2 = [SA(f"qlatT{i}", [128, 16, 128], BF16) for i in range(2)]
                      qidxT = SA("qidxT", [64, 8, 128], BF16)
                      score = SA("score", [128, L], F32)
                      rtmp = [SA(f"rtmp{i}", [128, L], F32) for i in range(2)]
                      mask = SA("mask", [128, max(L, 2048)], BF16)
                      MK = [('mask', 0), ('mask', 1)]
                      junkS = mask
                      maskT2 = [SA(f"maskT{i}", [128, NT, 128], BF16) for i in range(2)]
                      Eb = [SA(f"Eb{i}", [128, 8, 128], BF16) for i in range(2)]
                      oTn = SA("oTn", [128, 8, 128], BF16)
                      bsq = SA("bsq", [128, 8], F32)
                      wh = SA("wh", [128, NIT + 1], F32)

                      def stage_Q(qi):
                          i = c * 4 + qi
                          nj = i + 1
                          Li = nj * 128
                          qs = slice(qi * 128, (qi + 1) * 128)
                          qlatT = qlatT2[qi % 2]
                          par = qi % 2
                          for mg in range(2):
                              bk = psalloc(1)
                              fns = []
                              for m in range(4):
                                  for kk in range(2):
                                      fns.append(lambda e, m=m, kk=kk, mg=mg, bk=bk: e.matmul(ps[:, bk, m * 128:(m + 1) * 128], lhsT=wuq[:, kk, (mg * 4 + m) * 128:(mg * 4 + m + 1) * 128],
                                                                                          rhs=cqT[:, kk, qs], start=(kk == 0), stop=(kk == 1)))
                              p.group('pe', fns, reads=[('cqT', qi), 'wres'], writes=PK(bk))
                              p.op('act', lambda e, mg=mg, bk=bk: e.activation(out=qT[:, mg * 4:(mg + 1) * 4, :].rearrange("p m t -> p (m t)"), in_=ps[:, bk, :], func=AF.Copy),
                                   reads=PK(bk), writes=[('qT', mg)])
                          for hg in range(4):
                              bk = psalloc(1)
                              fns = [lambda e, h=h, hg=hg, bk=bk: e.matmul(ps[:, bk, h * 128:(h + 1) * 128], lhsT=wukp[:, hg * 4 + h, :], rhs=qT[:, (hg * 4 + h) // 2, :],
                                                                       start=True, stop=True) for h in range(4)]
                              p.group('pe', fns, reads=[('qT', 0), ('qT', 1), 'wres'], writes=PK(bk))
                              p.op('act', lambda e, hg=hg, bk=bk: e.activation(out=qlatT[:, hg * 4:(hg + 1) * 4, :].rearrange("p m t -> p (m t)"), in_=ps[:, bk, :], func=AF.Copy),
                                   reads=PK(bk), writes=[('qlatT', par, hg)])
                          for hg in range(2):
                              bk = psalloc(1)
                              fns = []
                              for h in range(4):
                                  for kk in range(2):
                                      fns.append(lambda e, h=h, kk=kk, hg=hg, bk=bk: e.matmul(ps[0:64, bk, h * 128:(h + 1) * 128], lhsT=wqi[:, kk, (hg * 4 + h) * 64:(hg * 4 + h + 1) * 64],
                                                                                          rhs=cqT[:, kk, qs], start=(kk == 0), stop=(kk == 1)))
                              p.group('pe', fns, reads=[('cqT', qi), 'wres'], writes=PK(bk))
                              p.op('act', lambda e, hg=hg, bk=bk: e.activation(out=qidxT[0:64, hg * 4:(hg + 1) * 4, :].rearrange("p m t -> p (m t)"), in_=ps[0:64, bk, :], func=AF.Copy),
                                   reads=PK(bk), writes=[('qidxT', hg)])
                          nsc = (Li + 511) // 512
                          nal = 1 if nsc == 1 else (2 if nsc == 2 else 4)
                          kread = [('kidxT', T) for T in range(nj)]
                          for h in range(8):
                              b0 = psalloc(nal)
                              fns = []
                              for sc in range(nsc):
                                  w = min(512, Li - sc * 512)
                                  fns.append(lambda e, sc=sc, w=w, b0=b0, h=h: e.matmul(ps[:, b0 + sc, 0:w], lhsT=qidxT[0:64, h, :], rhs=kidxT[0:64, sc * 512:sc * 512 + w],
                                                                                     start=True, stop=True))
                              p.group('pe', fns, reads=[('qidxT', 0), ('qidxT', 1)] + kread, writes=PK(b0, nal))
                              s = h % 2
                              psv = ps[:, b0:b0 + nsc, :].rearrange("p a b -> p (a b)")[:, 0:Li]
                              p.op('act', lambda e, s=s, psv=psv: e.activation(out=rtmp[s][:, 0:Li], in_=psv, func=AF.Relu), reads=PK(b0, nal), writes=[('rtmp', s)])
                              if h == 0:
                                  p.op('dve', lambda e, s=s, i=i: e.tensor_scalar(out=score[:, 0:Li], in0=rtmp[s][:, 0:Li], scalar1=widx[:, i, 0:1], scalar2=None, op0=ALU.mult),
                                       reads=[('rtmp', s), ('widx', i)], writes=['score'])
                              else:
                                  p.op('dve', lambda e, s=s, i=i, h=h: e.scalar_tensor_tensor(out=score[:, 0:Li], in0=rtmp[s][:, 0:Li], scalar=widx[:, i, h:h + 1], in1=score[:, 0:Li],
                                                                                            op0=ALU.mult, op1=ALU.add),
                                       reads=[('rtmp', s), ('widx', i), 'score'], writes=['score'])
                          p.op('dve', lambda e: e.tensor_reduce(out=bsq[:, 0:1], in_=score[:, 0:Li], axis=AX.X, op=ALU.min), reads=['score'], writes=[('b', 'lo')])
                          p.op('dve', lambda e: e.memset(score[0:64, Li - 64:Li], NEG), reads=[('b', 'lo')], writes=['score'])
                          p.op('dve', lambda e: e.tensor_reduce(out=bsq[:, 5:6], in_=score[:, 0:Li], axis=AX.X, op=ALU.max), reads=['score'], writes=[('b', 'hi')])
                          p.op('dve', lambda e: e.tensor_tensor(out=bsq[:, 1:2], in0=bsq[:, 5:6], in1=bsq[:, 0:1], op=ALU.subtract),
                               reads=[('b', 'hi'), ('b', 'lo')], writes=[('b', 'w0')])
                          p.op('dve', lambda e: e.scalar_tensor_tensor(out=bsq[:, 0:1], in0=bsq[:, 1:2], scalar=-(2.0 ** -10), in1=bsq[:, 0:1], op0=ALU.mult, op1=ALU.add),
                               reads=[('b', 'w0'), ('b', 'lo')], writes=[('b', 'lo')])
                          p.op('dve', lambda e: e.tensor_scalar(out=bsq[:, 0:1], in0=bsq[:, 0:1], scalar1=-1e-6, scalar2=None, op0=ALU.add),
                               reads=[('b', 'lo')], writes=[('b', 'lo')])
                          p.op('dve', lambda e: e.tensor_tensor(out=bsq[:, 1:2], in0=bsq[:, 5:6], in1=bsq[:, 0:1], op=ALU.subtract),
                               reads=[('b', 'hi'), ('b', 'lo')], writes=[('b', 'w0')])
                          p.op('dve', lambda e: e.tensor_scalar(out=wh[:, :], in0=pw2[:, :], scalar1=bsq[:, 1:2], scalar2=None, op0=ALU.mult),
                               reads=[('b', 'w0'), 'pw2'], writes=['wh'])
                          p.op('dve', lambda e: e.tensor_tensor(out=bsq[:, 7:8], in0=bsq[:, 0:1], in1=wh[:, 0:1], op=ALU.add),
                               reads=[('b', 'lo'), 'wh'], writes=[('b', 'mid')])

                      def stage_B(qi):
                          i = c * 4 + qi
                          nj = i + 1
                          Li = nj * 128
                          maskT = maskT2[qi % 2]
                          par = qi % 2
                          frac = ACT_FRAC_B0 if qi == 0 else ACT_FRAC
                          H = max(64, min(Li - 64, ((int(Li * frac) + 63) // 64) * 64))
                          thr2 = float(2 * TOPK - H) - 0.5
                          if Li <= TOPK:
                              yield
                              return
                          for n in range(NIT + 1):
                              if n > 0:
                                  m = n - 1
                                  p.op('dve', lambda e: e.scalar_tensor_tensor(out=bsq[:, 5:6], in0=bsq[:, 4:5], scalar=2.0, in1=bsq[:, 3:4], op0=ALU.mult, op1=ALU.subtract),
                                       reads=[('b', 'c1'), ('b', 'c2')], writes=[('b', 'tot')])
                                  p.op('dve', lambda e, m=m: e.scalar_tensor_tensor(out=bsq[:, 6:7], in0=bsq[:, 5:6], scalar=thr2, in1=wh[:, m:m + 1], op0=ALU.is_ge, op1=ALU.mult),
                                       reads=[('b', 'tot'), 'wh'], writes=[('b', 'tmp2')])
                                  p.op('dve', lambda e, m=m: e.scalar_tensor_tensor(out=bsq[:, 7:8], in0=bsq[:, 6:7], scalar=bsq[:, 7:8], in1=wh[:, m + 1:m + 2], op0=ALU.add, op1=ALU.subtract),
                                       reads=[('b', 'tmp2'), ('b', 'mid'), 'wh'], writes=[('b', 'mid')])
                              yield
                              if n < NIT:
                                  p.op('act', lambda e: e.activation(out=mask[:, 0:H], in_=score[:, 0:H], func=AF.Sign, bias=bsq[:, 7:8], scale=-1.0, accum_out=bsq[:, 3:4]),
                                       reads=['score', ('b', 'mid')], writes=[('b', 'c1'), ('mask', 0)])
                                  p.op('dve', lambda e: e.tensor_scalar(out=mask[:, H:Li], in0=score[:, H:Li], scalar1=bsq[:, 7:8], scalar2=0.0, op0=ALU.is_ge, op1=ALU.add,
                                                                        accum_out=bsq[:, 4:5]),
                                       reads=['score', ('b', 'mid')], writes=[('b', 'c2'), ('mask', 1)])
                              yield
                          p.op('dve', lambda e: e.tensor_tensor(out=bsq[:, 0:1], in0=bsq[:, 7:8], in1=wh[:, NIT:NIT + 1], op=ALU.subtract),
                               reads=[('b', 'mid'), 'wh'], writes=[('b', 'lo')])
                          yield

                      def stage_Bfin(qi):
                          i = c * 4 + qi
                          nj = i + 1
                          Li = nj * 128
                          maskT = maskT2[qi % 2]
                          par = qi % 2
                          p.op('dve', lambda e: e.tensor_scalar(out=mask[:, 0:Li], in0=score[:, 0:Li], scalar1=bsq[:, 0:1], scalar2=None, op0=ALU.is_ge),
                               reads=['score', ('b', 'lo')], writes=MK)
                          for j0 in range(0, nj, 8):
                              n8 = min(8, nj - j0)
                              bt = psalloc(1)
                              while bt >= 4:
                                  bt = psalloc(1)
                              pt = psT(bt)
                              p.group('pe', [lambda e, jj=jj, j0=j0, pt=pt: e.transpose(out=pt[:, jj * 128:(jj + 1) * 128], in_=mask[:, (j0 + jj) * 128:(j0 + jj + 1) * 128], identity=ident[:])
                                             for jj in range(n8)], reads=MK + ['ident'], writes=PK(bt))
                              p.op('dve', lambda e, j0=j0, n8=n8, pt=pt: e.tensor_scalar(out=maskT[:, j0:j0 + n8, :].rearrange("p a b -> p (a b)"), in0=pt[:, 0:n8 * 128],
                                                                                        scalar1=-1.0, scalar2=30000.0, op0=ALU.add, op1=ALU.mult),
                                   reads=PK(bt), writes=[('maskT', par)])

                      def stage_T(qi):
                          i = c * 4 + qi
                          nj = i + 1
                          qs = slice(qi * 128, (qi + 1) * 128)
                          qlatT = qlatT2[qi % 2]
                          maskT = maskT2[qi % 2]
                          par = qi % 2
                          for hf in range(2):
                              def qk_step(j, hf=hf):
                                  sb = 0
                                  fns = []
                                  for g in range(2):
                                      fns.append(lambda e, g=g, j=j, sb=sb, hf=hf: e.matmul(ps[:, sb + g, :], lhsT=ckvT[:, j * 128:(j + 1) * 128],
                                                                                         rhs=qlatT[:, hf * 8 + g * 4:hf * 8 + g * 4 + 4, :], start=True, stop=False))
                                      fns.append(lambda e, g=g, j=j, sb=sb: e.matmul(ps[:, sb + g, :], lhsT=ident[:],
                                                                                  rhs=maskT[:, j, :].unsqueeze(1).broadcast_to([128, 4, 128]), start=False, stop=True))
                                  p.group('pe', fns, reads=[('ckvT', j), ('maskT', par), 'ident'] + [('qlatT', par, hg) for hg in range(4)], writes=PK(sb, 2))

                              def exp_step(j, hf=hf):
                                  sb = 0
                                  s = j % 2
                                  p.op('act', lambda e, sb=sb, s=s: e.activation(out=Eb[s][:, :, :].rearrange("p a b -> p (a b)"), in_=ps[:, sb:sb + 2, :].rearrange("p a b -> p (a b)"),
                                                                              func=AF.Exp, scale=0.125),
                                       reads=PK(sb, 2), writes=[('Eb', s)])

                              def rest_step(j, hf=hf):
                                  s = j % 2
                                  fns = []
                                  for g in range(2):
                                      fns.append(lambda e, g=g, j=j, s=s: e.matmul(ps[:, OAC_B + g, :], lhsT=ckv_tok[:, j, :], rhs=Eb[s][:, g * 4:(g + 1) * 4, :], start=(j == 0), stop=(j == nj - 1)))
                                      fns.append(lambda e, g=g, j=j, s=s: e.matmul(ps[:, DEN_B + g, :], lhsT=ones[:], rhs=Eb[s][:, g * 4:(g + 1) * 4, :], start=(j == 0), stop=(j == nj - 1)))
                                  p.group('pe', fns, reads=[('ckv_tok', j), ('Eb', s), 'ones'], writes=PK(OAC_B, 4))

                              qk_step(0)
                              for j in range(nj):
                                  exp_step(j)
                                  if j + 1 < nj:
                                      qk_step(j + 1)
                                  yield
                                  rest_step(j)
                                  yield
                              p.op('act', lambda e: e.activation(out=rtmp[1][:, 0:1024], in_=ps[:, DEN_B:DEN_B + 2, :].rearrange("p a b -> p (a b)"), func=AF.Ln), reads=PK(DEN_B, 2), writes=[('rtmp', 1)])
                              p.op('act', lambda e: e.activation(out=rtmp[1][:, 0:1024], in_=rtmp[1][:, 0:1024], func=AF.Exp, scale=-1.0), reads=[('rtmp', 1)], writes=[('rtmp', 1)])
                              p.op('dve', lambda e: e.tensor_tensor(out=oTn[:, :, :].rearrange("p a b -> p (a b)"), in0=ps[:, OAC_B:OAC_B + 2, :].rearrange("p a b -> p (a b)"), in1=rtmp[1][:, 0:1024], op=ALU.mult),
                                   reads=PK(OAC_B, 2) + [('rtmp', 1)], writes=['oTn'])
                              bk = 7
                              fns = []
                              for pr in range(4):
                                  h0 = hf * 8 + 2 * pr
                                  fns.append(lambda e, pr=pr, h0=h0, bk=bk: e.matmul(ps[:, bk, pr * 128:(pr + 1) * 128], lhsT=wuvp[:, h0, :], rhs=oTn[:, 2 * pr, :], start=True, stop=False))
                                  fns.append(lambda e, pr=pr, h0=h0, bk=bk: e.matmul(ps[:, bk, pr * 128:(pr + 1) * 128], lhsT=wuvp[:, h0 + 1, :], rhs=oTn[:, 2 * pr + 1, :], start=False, stop=True))
                              p.group('pe', fns, reads=['oTn', 'wres'], writes=PK(bk))
                              p.op('dve', lambda e, bk=bk, hf=hf: e.tensor_copy(out=oT[:, hf * 4:(hf + 1) * 4, qs], in_=ps[:, bk, :].rearrange("p (m t) -> p m t", m=4)),
                                   reads=PK(bk), writes=[('oT', qi)])
                              yield
                              yield

                      def interleave(ga, gb):
                          la = list_len[0]
                          alive_a, alive_b = ga is not None, gb is not None
                          while alive_a or alive_b:
                              if alive_b:
                                  try:
                                      next(gb)
                                  except StopIteration:
                                      alive_b = False
                              if alive_a:
                                  try:
                                      next(ga)
                                  except StopIteration:
                                      alive_a = False

                      list_len = [0]

                      def dbg(tag):
                          if DBG and DBG == f"{c},{tag}":
                              p.dead = True

                      with ExitStack() as esf:
                          SF = lambda name, shape, dt: _alloc(esf, name, shape, dt)
                          xs = [mask[:, i * 1024:(i + 1) * 1024] for i in range(2)]
                          yT = SF("yT", [128, 8, 512], BF16)
                          y2T = SF("y2T", [128, 8, 512], BF16)
                          sig = [SF(f"sig{i}", [128, 512], F32) for i in range(2)]
                          mean = SF("mean", [128, 512], F32)
                          rstd = SF("rstd", [128, 512], F32)
                          tmpa = [SF(f"tmpa{i}", [128, 512], F32) for i in range(2)]
                          cqn = SF("cqn", [128, 4, 256], BF16)
                          kn = SF("kn", [128, 4, 64], BF16)
                          st2 = SF("st2", [128, 64], F32)
                          smp = rtmp[1][:, 0:456]
                          if c == 0:
                              p.op('dve', lambda e: e.memset(vT[:, :, 0:30], 0.0), writes=['vT'])
                          for tt in range(4):
                              T = c * 4 + tt
                              p.dma('sp', xt[tt][:], x[b, T * 128:(T + 1) * 128, :], writes=[('xt', tt)])
                              sq_accum(xt[tt][:], 1024, st[:, tt:tt + 1], [('xt', tt)], [('st', 'ssq', tt)])
                          p.op('dve', lambda e: e.tensor_scalar(out=st[:, 4:8], in0=st[:, 0:4], scalar1=1.0 / D, scalar2=EPS, op0=ALU.mult, op1=ALU.add),
                               reads=[('st', 'ssq', i) for i in range(4)], writes=[('st', 'ms')])
                          rsqrt(st[:, 8:12], st[:, 4:8], [('st', 'ms')], [('st', 'rstd')])
                          if limit < 1.07:
                              p.dead = True
                          for tt in range(4):
                              s = tt % 2
                              p.op('dve', lambda e, tt=tt, s=s: e.tensor_scalar(out=xs[s], in0=xt[tt][:], scalar1=st[:, 8 + tt:9 + tt], scalar2=None, op0=ALU.mult),
                                   reads=[('xt', tt), ('st', 'rstd')], writes=[('mask', s)])
                              if limit < 1.08:
                                  p.dead = True
                              bk = psalloc(1)
                              pt = psT(bk)
                              p.group('pe', [lambda e, k=k, s=s, pt=pt: e.transpose(out=pt[:, k * 128:(k + 1) * 128], in_=xs[s][:, k * 128:(k + 1) * 128], identity=ident[:])
                                             for k in range(8)], reads=[('mask', s), 'ident'], writes=PK(bk))
                              if limit < 1.09:
                                  p.dead = True
                              p.op('dve', lambda e, tt=tt, pt=pt: e.tensor_copy(out=hT[:, :, tt * 128:(tt + 1) * 128], in_=pt.rearrange("p (k t) -> p k t", k=8)),
                                   reads=PK(bk), writes=[('hT', tt)])
                          hTk = [('hT', i) for i in range(4)]
                          if limit < 1.2:
                              p.dead = True
                          smps = [rtmp[tt // 2][:, (tt % 2) * 456:(tt % 2) * 456 + 456] for tt in range(4)]
                          smk = [('rtmp', tt // 2) for tt in range(4)]
                          for tt in range(4):
                              bk = psalloc(1)
                              p.group('pe', [lambda e, k=k, tt=tt, bk=bk: e.matmul(ps[:, bk, 0:456], lhsT=hT[:, k, tt * 128:(tt + 1) * 128], rhs=wsm[:, k, :],
                                                                                start=(k == 0), stop=(k == 7)) for k in range(8)],
                                      reads=[('hT', tt), 'wres'], writes=PK(bk))
                              p.op('dve', lambda e, bk=bk, tt=tt: e.tensor_copy(out=smps[tt], in_=ps[:, bk, 0:456]), reads=PK(bk), writes=[smk[tt]])
                              sq_accum(smps[tt][:, 0:256], 256, st2[:, 0 + tt:1 + tt], [smk[tt]], [('st2', 'q1', tt)])
                              sq_accum(smps[tt][:, 256:384], 128, st2[:, 4 + tt:5 + tt], [smk[tt]], [('st2', 'q2', tt)])
                              sq_accum(smps[tt][:, 384:448], 64, st2[:, 8 + tt:9 + tt], [smk[tt]], [('st2', 'q3', tt)])
                              p.op('dve', lambda e, tt=tt: e.tensor_reduce(out=st2[:, 12 + tt:13 + tt], in_=smps[tt][:, 384:448], axis=AX.X, op=ALU.add),
                                   reads=[smk[tt]], writes=[('st2', 'q4', tt)])
                          q1k = [('st2', 'q1', t) for t in range(4)]
                          q2k = [('st2', 'q2', t) for t in range(4)]
                          q3k = [('st2', 'q3', t) for t in range(4)]
                          q4k = [('st2', 'q4', t) for t in range(4)]
                          p.op('dve', lambda e: e.tensor_scalar(out=st2[:, 16:20], in0=st2[:, 0:4], scalar1=1.0 / 256, scalar2=EPS, op0=ALU.mult, op1=ALU.add),
                               reads=q1k, writes=[('st2', 'm1')])
                          p.op('dve', lambda e: e.tensor_scalar(out=st2[:, 20:24], in0=st2[:, 4:8], scalar1=1.0 / 128, scalar2=EPS, op0=ALU.mult, op1=ALU.add),
                               reads=q2k, writes=[('st2', 'm2')])
                          p.op('dve', lambda e: e.tensor_scalar(out=st2[:, 28:32], in0=st2[:, 12:16], scalar1=1.0 / 64, scalar2=None, op0=ALU.mult),
                               reads=q4k, writes=[('st2', 'mk')])
                          p.op('dve', lambda e: e.tensor_tensor(out=st2[:, 32:36], in0=st2[:, 28:32], in1=st2[:, 28:32], op=ALU.mult),
                               reads=[('st2', 'mk')], writes=[('st2', 'mk2')])
                          p.op('dve', lambda e: e.tensor_scalar(out=st2[:, 36:40], in0=st2[:, 8:12], scalar1=1.0 / 64, scalar2=EPS, op0=ALU.mult, op1=ALU.add),
                               reads=q3k, writes=[('st2', 'm3a')])
                          p.op('dve', lambda e: e.tensor_tensor(out=st2[:, 24:28], in0=st2[:, 36:40], in1=st2[:, 32:36], op=ALU.subtract),
                               reads=[('st2', 'm3a'), ('st2', 'mk2')], writes=[('st2', 'm3')])
                          rsqrt(st2[:, 40:52], st2[:, 16:28], [('st2', 'm1'), ('st2', 'm2'), ('st2', 'm3')], [('st2', 'r3')])
                          for tt in range(4):
                              T = c * 4 + tt
                              p.op('dve', lambda e, tt=tt: e.tensor_scalar(out=cqn[:, tt, :], in0=smps[tt][:, 0:256], scalar1=st2[:, 40 + tt:41 + tt], scalar2=None, op0=ALU.mult),
                                   reads=[smk[tt], ('st2', 'r3')], writes=[('cqn', tt)])
                              p.op('dve', lambda e, tt=tt, T=T: e.scalar_tensor_tensor(out=ckv_tok[:, T, :], in0=smps[tt][:, 256:384], scalar=st2[:, 44 + tt:45 + tt], in1=kvg_bc,
                                                                                      op0=ALU.mult, op1=ALU.mult),
                                   reads=[smk[tt], ('st2', 'r3'), 'bcs'], writes=[('ckv_tok', T)])
                              p.op('dve', lambda e, tt=tt: e.tensor_scalar(out=kn[:, tt, :], in0=smps[tt][:, 384:448], scalar1=st2[:, 28 + tt:29 + tt], scalar2=st2[:, 48 + tt:49 + tt],
                                                                           op0=ALU.subtract, op1=ALU.mult),
                                   reads=[smk[tt], ('st2', 'r3'), ('st2', 'mk')], writes=[('kn', tt)])
                              p.op('dve', lambda e, tt=tt, T=T: e.tensor_scalar(out=widx[:, T, :], in0=smps[tt][:, 448:456], scalar1=float((8 * 64) ** -0.5), scalar2=None, op0=ALU.mult),
                                   reads=[smk[tt]], writes=[('widx', T)])
                              bt = psalloc(1)
                              pt = psT(bt)
                              p.group('pe', [
                                  lambda e, pt=pt, tt=tt: e.transpose(out=pt[:, 0:128], in_=cqn[:, tt, 0:128], identity=ident[:]),
                                  lambda e, pt=pt, tt=tt: e.transpose(out=pt[:, 128:256], in_=cqn[:, tt, 128:256], identity=ident[:]),
                                  lambda e, pt=pt, T=T: e.transpose(out=pt[:, 256:384], in_=ckv_tok[:, T, :], identity=ident[:]),
                                  lambda e, pt=pt, tt=tt: e.transpose(out=pt[0:64, 384:512], in_=kn[:, tt, 0:64], identity=ident[:]),
                              ], reads=[('cqn', tt), ('ckv_tok', T), ('kn', tt), 'ident'], writes=PK(bt))
                              p.op('dve', lambda e, pt=pt, tt=tt: e.tensor_copy(out=cqT[:, :, tt * 128:(tt + 1) * 128], in_=pt[:, 0:256].rearrange("p (k t) -> p k t", k=2)),
                                   reads=PK(bt), writes=[('cqT', tt)])
                              p.op('dve', lambda e, pt=pt, T=T: e.tensor_copy(out=ckvT[:, T * 128:(T + 1) * 128], in_=pt[:, 256:384]),
                                   reads=PK(bt), writes=[('ckvT', T)])
                              p.op('dve', lambda e, pt=pt, T=T: e.tensor_scalar(out=kidxT[0:64, T * 128:(T + 1) * 128], in0=pt[0:64, 384:512],
                                                                               scalar1=cols[0:64, C_KIG:C_KIG + 1], scalar2=cols[0:64, C_KIB:C_KIB + 1], op0=ALU.mult, op1=ALU.add),
                                   reads=PK(bt) + ['cols'], writes=[('kidxT', T)])
                          if limit < 1.3:
                              p.dead = True
                          stage_Q(0)
                          B0 = stage_B(0)

                          def tick():
                              try:
                                  next(B0)
                              except StopIteration:
                                  pass

                          for half in range(2):
                              Wa, ka = wload(wcols(s_win, half * 512), v_k512)
                              Wg, kg = wload(wcols(s_win, 1024 + half * 512), v_k512)
                              for ctl in range(4):
                                  ct = half * 4 + ctl
                                  ba = psalloc(1)
                                  p.group('pe', [lambda e, k=k, ctl=ctl, ba=ba, Wa=Wa: e.matmul(ps[:, ba, :], lhsT=Wa[:, k, ctl * 128:(ctl + 1) * 128], rhs=hT[:, k, :],
                                                                                            start=(k == 0), stop=(k == 7)) for k in range(8)],
                                          reads=hTk + [ka], writes=PK(ba))
                                  bg = psalloc(1)
                                  p.group('pe', [lambda e, k=k, ctl=ctl, bg=bg, Wg=Wg: e.matmul(ps[:, bg, :], lhsT=Wg[:, k, ctl * 128:(ctl + 1) * 128], rhs=hT[:, k, :],
                                                                                            start=(k == 0), stop=(k == 7)) for k in range(8)],
                                          reads=hTk + [kg], writes=PK(bg))
                                  s = ct % 2
                                  p.op('act', lambda e, bg=bg, s=s: e.activation(out=sig[s][:], in_=ps[:, bg, :], func=AF.Sigmoid), reads=PK(bg), writes=[('sig', s)])
                                  p.op('dve', lambda e, ba=ba, s=s, ct=ct: e.tensor_tensor(out=vT[:, ct, 30:542], in0=ps[:, ba, :], in1=sig[s][:], op=ALU.mult),
                                       reads=PK(ba) + [('sig', s)], writes=[('vT', ct)])
                                  tick()
                          if limit < 1.4:
                              p.dead = True
                          for ct in range(8):
                              Dg, kd = wload(s_dg[ct], v_dg)
                              by = psalloc(1)
                              p.group('pe', [lambda e, k=k, ct=ct, by=by, Dg=Dg: e.matmul(ps[:, by, :], lhsT=Dg[:, k, :], rhs=vT[:, ct, k:k + 512],
                                                                                        start=(k == 0), stop=(k == 30)) for k in range(31)],
                                      reads=[('vT', ct), 'vT', kd], writes=PK(by))
                              p.op('act', lambda e, by=by, ct=ct: e.activation(out=yT[:, ct, :], in_=ps[:, by, :], func=AF.Identity, bias=cols[:, C_DWB + ct:C_DWB + ct + 1]),
                                   reads=PK(by) + ['cols'], writes=[('yT', ct)])
                              p.op('act', lambda e, by=by, ct=ct: e.activation(out=y2T[:, ct, :], in_=ps[:, by, :], func=AF.Square, bias=cols[:, C_DWB + ct:C_DWB + ct + 1]),
                                   reads=PK(by) + ['cols'], writes=[('y2T', ct)])
                              tick()
                          if limit < 1.5:
                              p.dead = True
                          if c + 1 < NCH:
                              p.op('dve', lambda e: e.tensor_copy(out=vT[:, :, 0:30], in_=vT[:, :, 512:542]),
                                   reads=[('vT', ct) for ct in range(8)], writes=['vT'])
                          bs = psalloc(1)
                          p.group('pe', [lambda e, ct=ct, bs=bs: e.matmul(ps[:, bs, :], lhsT=ones[:], rhs=yT[:, ct, :], start=(ct == 0), stop=(ct == 7)) for ct in range(8)],
                                  reads=[('yT', ct) for ct in range(8)] + ['ones'], writes=PK(bs))
                          bq = psalloc(1)
                          p.group('pe', [lambda e, ct=ct, bq=bq: e.matmul(ps[:, bq, :], lhsT=ones[:], rhs=y2T[:, ct, :], start=(ct == 0), stop=(ct == 7)) for ct in range(8)],
                                  reads=[('y2T', ct) for ct in range(8)] + ['ones'], writes=PK(bq))
                          p.op('act', lambda e, bs=bs: e.activation(out=mean[:], in_=ps[:, bs, :], func=AF.Identity, scale=1.0 / D), reads=PK(bs), writes=['mean'])
                          p.op('dve', lambda e: e.tensor_tensor(out=tmpa[0][:], in0=mean[:], in1=mean[:], op=ALU.mult), reads=['mean'], writes=[('tmpa', 0)])
                          p.op('dve', lambda e, bq=bq: e.scalar_tensor_tensor(out=tmpa[1][:], in0=ps[:, bq, :], scalar=1.0 / D, in1=tmpa[0][:], op0=ALU.mult, op1=ALU.subtract),
                               reads=PK(bq) + [('tmpa', 0)], writes=[('tmpa', 1)])
                          p.op('dve', lambda e: e.tensor_scalar(out=tmpa[0][:], in0=tmpa[1][:], scalar1=EPS, scalar2=None, op0=ALU.add),
                               reads=[('tmpa', 1)], writes=[('tmpa', 0)])
                          rsqrt(rstd[:], tmpa[0][:], [('tmpa', 0)], ['rstd'])
                          for ct in range(8):
                              s = ct % 2
                              p.op('dve', lambda e, ct=ct, s=s: e.tensor_tensor(out=tmpa[s][:], in0=yT[:, ct, :], in1=mean[:], op=ALU.subtract),
                                   reads=[('yT', ct), 'mean'], writes=[('tmpa', s)])
                              p.op('dve', lambda e, s=s: e.tensor_tensor(out=sig[s][:], in0=tmpa[s][:], in1=rstd[:], op=ALU.mult),
                                   reads=[('tmpa', s), 'rstd'], writes=[('sig', s)])
                              p.op('act', lambda e, ct=ct, s=s: e.activation(out=yT[:, ct, :], in_=sig[s][:], func=AF.Silu, scale=cols[:, C_LNG + ct:C_LNG + ct + 1],
                                                                          bias=cols[:, C_LNB + ct:C_LNB + ct + 1]),
                                   reads=[('sig', s), 'cols'], writes=[('yT', ct)])
                              tick()
                          sTk = [('yT', ct) for ct in range(8)]
                          if limit < 1.6:
                              p.dead = True
                          for half in range(2):
                              Wp, kp = wload(wcols(s_wco, half * 512), v_k512)
                              Wg, kg = wload(wcols(s_win, 2504 + half * 512), v_k512)
                              for dl in range(4):
                                  dt = half * 4 + dl
                                  ba = psalloc(1)
                                  p.group('pe', [lambda e, k=k, dl=dl, ba=ba, Wp=Wp: e.matmul(ps[:, ba, :], lhsT=Wp[:, k, dl * 128:(dl + 1) * 128], rhs=yT[:, k, :],
                                                                                           start=(k == 0), stop=(k == 7)) for k in range(8)],
                                          reads=sTk + [kp], writes=PK(ba))
                                  bg = psalloc(1)
                                  p.group('pe', [lambda e, k=k, dl=dl, bg=bg, Wg=Wg: e.matmul(ps[:, bg, :], lhsT=Wg[:, k, dl * 128:(dl + 1) * 128], rhs=hT[:, k, :],
                                                                                           start=(k == 0), stop=(k == 7)) for k in range(8)],
                                          reads=hTk + [kg], writes=PK(bg))
                                  s = dt % 2
                                  p.op('act', lambda e, bg=bg, s=s, dt=dt: e.activation(out=tmpa[s][:], in_=ps[:, bg, :], func=AF.Sigmoid, bias=cols[:, C_BGATE + dt:C_BGATE + dt + 1]),
                                       reads=PK(bg) + ['cols'], writes=[('tmpa', s)])
                                  p.op('dve', lambda e, ba=ba, s=s, dt=dt: e.tensor_tensor(out=mergedT[:, dt, :], in0=ps[:, ba, :], in1=tmpa[s][:], op=ALU.mult),
                                       reads=PK(ba) + [('tmpa', s)], writes=[('mT', dt)])
                                  tick()
                          for _ in B0:
                              pass
                      if limit < 2:
                          p.dead = True
                      stage_Bfin(0)
                      dbg("B0")
                      for qi in range(4):
                          if qi + 1 < 4:
                              stage_Q(qi + 1)
                          dbg(f"Q{qi + 1}")
                          interleave(stage_T(qi), stage_B(qi + 1) if qi + 1 < 4 else None)
                          if qi + 1 < 4:
                              stage_Bfin(qi + 1)
                          dbg(f"T{qi}")
                      p.barrier()
                  if limit < 3:
                      p.dead = True
                  with ExitStack() as esb:
                      SB_ = lambda name, shape, dt: _alloc(esb, name, shape, dt)
                      h1T = SB_("h1T", [128, 32, 512], BF16)
                      tmpb = [SB_(f"tmpb{i}", [128, 512], F32) for i in range(2)]
                      xs2 = [SB_(f"xs2_{i}", [128, D], BF16) for i in range(2)]
                      oTk = [('oT', qi) for qi in range(4)]
                      hTk = [('hT', i) for i in range(4)]
                      for half in range(2):
                          Wp, kp = wload(wcols(s_wao, half * 512), v_k512)
                          Wg, kg = wload(wcols(s_win, 3528 + half * 512), v_k512)
                          for dl in range(4):
                              dt = half * 4 + dl
                              ba = psalloc(1)
                              p.group('pe', [lambda e, k=k, dl=dl, ba=ba, Wp=Wp: e.matmul(ps[:, ba, :], lhsT=Wp[:, k, dl * 128:(dl + 1) * 128], rhs=oT[:, k, :],
                                                                                       start=(k == 0), stop=(k == 7)) for k in range(8)],
                                      reads=oTk + [kp], writes=PK(ba))
                              bg = psalloc(1)
                              p.group('pe', [lambda e, k=k, dl=dl, bg=bg, Wg=Wg: e.matmul(ps[:, bg, :], lhsT=Wg[:, k, dl * 128:(dl + 1) * 128], rhs=hT[:, k, :],
                                                                                       start=(k == 0), stop=(k == 7)) for k in range(8)],
                                      reads=hTk + [kg], writes=PK(bg))
                              s = dt % 2
                              p.op('act', lambda e, bg=bg, s=s, dt=dt: e.activation(out=tmpb[s][:], in_=ps[:, bg, :], func=AF.Sigmoid, bias=cols[:, C_BGATE + 8 + dt:C_BGATE + 9 + dt]),
                                   reads=PK(bg) + ['cols'], writes=[('tmpb', s)])
                              p.op('dve', lambda e, ba=ba, s=s: e.tensor_tensor(out=tmpb[s][:], in0=ps[:, ba, :], in1=tmpb[s][:], op=ALU.mult),
                                   reads=PK(ba) + [('tmpb', s)], writes=[('tmpb', s)])
                              p.op('dve', lambda e, s=s, dt=dt: e.tensor_tensor(out=mergedT[:, dt, :], in0=mergedT[:, dt, :], in1=tmpb[s][:], op=ALU.add),
                                   reads=[('tmpb', s), ('mT', dt)], writes=[('mT', dt)])
                      mTk = [('mT', dt) for dt in range(8)]
                      Wo = [wload(wcols(s_wo, dh * 512), v_k512) for dh in range(2)]
                      for tt in range(4):
                          for dh in range(2):
                              bk = psalloc(1)
                              W_, kw = Wo[dh]
                              p.group('pe', [lambda e, k=k, tt=tt, bk=bk, W_=W_: e.matmul(ps[:, bk, :], lhsT=mergedT[:, k, tt * 128:(tt + 1) * 128], rhs=W_[:, k, :],
                                                                                       start=(k == 0), stop=(k == 7)) for k in range(8)],
                                      reads=mTk + [kw], writes=PK(bk))
                              p.op('dve', lambda e, tt=tt, dh=dh, bk=bk: e.tensor_tensor(out=xt[tt][:, dh * 512:(dh + 1) * 512], in0=ps[:, bk, :], in1=xt[tt][:, dh * 512:(dh + 1) * 512], op=ALU.add),
                                   reads=PK(bk) + [('xt', tt)], writes=[('xt', tt)])
                          sq_accum(xt[tt][:], 1024, st[:, 32 + tt:33 + tt], [('xt', tt)], [('st', 'ssq2', tt)])
                      p.op('dve', lambda e: e.tensor_scalar(out=st[:, 36:40], in0=st[:, 32:36], scalar1=1.0 / D, scalar2=EPS, op0=ALU.mult, op1=ALU.add),
                           reads=[('st', 'ssq2', i) for i in range(4)], writes=[('st', 'ms2')])
                      rsqrt(st[:, 40:44], st[:, 36:40], [('st', 'ms2')], [('st', 'rstd2')])
                      for tt in range(4):
                          s = tt % 2
                          p.op('dve', lambda e, tt=tt, s=s: e.tensor_scalar(out=xs2[s][:], in0=xt[tt][:], scalar1=st[:, 40 + tt:41 + tt], scalar2=None, op0=ALU.mult),
                               reads=[('xt', tt), ('st', 'rstd2')], writes=[('xs2', s)])
                          bk = psalloc(1)
                          pt = psT(bk)
                          p.group('pe', [lambda e, k=k, s=s, pt=pt: e.transpose(out=pt[:, k * 128:(k + 1) * 128], in_=xs2[s][:, k * 128:(k + 1) * 128], identity=ident[:])
                                         for k in range(8)], reads=[('xs2', s), 'ident'], writes=PK(bk))
                          p.op('dve', lambda e, tt=tt, pt=pt: e.tensor_copy(out=hT[:, :, tt * 128:(tt + 1) * 128], in_=pt.rearrange("p (k t) -> p k t", k=8)),
                               reads=PK(bk), writes=[('hT', tt)])
                      for fc in range(8):
                          W1, k1 = wload(wcols(s_w1, fc * 512), v_k512)
                          for fl in range(4):
                              f = fc * 4 + fl
                              bk = psalloc(1)
                              p.group('pe', [lambda e, k=k, fl=fl, bk=bk, W1=W1: e.matmul(ps[:, bk, :], lhsT=W1[:, k, fl * 128:(fl + 1) * 128], rhs=hT[:, k, :],
                                                                                       start=(k == 0), stop=(k == 7)) for k in range(8)],
                                      reads=hTk + [k1], writes=PK(bk))
                              s = f % 2
                              p.op('act', lambda e, bk=bk, s=s: e.activation(out=tmpb[s][:], in_=ps[:, bk, :], func=AF.Relu), reads=PK(bk), writes=[('tmpb', s)])
                              p.op('dve', lambda e, s=s, f=f: e.tensor_tensor(out=h1T[:, f, :], in0=tmpb[s][:], in1=tmpb[s][:], op=ALU.mult),
                                   reads=[('tmpb', s)], writes=[('h1T', f)])
                      for dh in range(2):
                          for fc in range(4):
                              W2, k2 = wload(s_w2[fc * 1024:(fc + 1) * 1024, dh * 512:(dh + 1) * 512].rearrange("(k p) c -> p k c", p=128), v_k512)
                              fns = []
                              for fl in range(8):
                                  f = fc * 8 + fl
                                  for tt in range(4):
                                      fns.append(lambda e, f=f, fl=fl, tt=tt, W2=W2: e.matmul(ps[:, tt, :], lhsT=h1T[:, f, tt * 128:(tt + 1) * 128], rhs=W2[:, fl, :],
                                                                                           start=(f == 0), stop=(f == 31)))
                              p.group('pe', fns, reads=[('h1T', fc * 8 + fl) for fl in range(8)] + [k2], writes=PK(0, 4))
                          for tt in range(4):
                              p.op('dve', lambda e, tt=tt, dh=dh: e.tensor_tensor(out=xt[tt][:, dh * 512:(dh + 1) * 512], in0=ps[:, tt, :], in1=xt[tt][:, dh * 512:(dh + 1) * 512], op=ALU.add),
                                   reads=PK(tt) + [('xt', tt)], writes=[('xt', tt)])
                      psr[0] = 4
                      for tt in range(4):
                          sq_accum(xt[tt][:], 1024, st[:, 44 + tt:45 + tt], [('xt', tt)], [('st', 'ssq3', tt)])
                      p.op('dve', lambda e: e.tensor_scalar(out=st[:, 48:52], in0=st[:, 44:48], scalar1=1.0 / D, scalar2=EPS, op0=ALU.mult, op1=ALU.add),
                           reads=[('st', 'ssq3', i) for i in range(4)], writes=[('st', 'ms3')])
                      rsqrt(st[:, 52:56], st[:, 48:52], [('st', 'ms3')], [('st', 'rstd3')])
                      for tt in range(4):
                          T = c * 4 + tt
                          p.op('dve', lambda e, tt=tt: e.scalar_tensor_tensor(out=xt[tt][:], in0=xt[tt][:], scalar=st[:, 52 + tt:53 + tt], in1=gfin_bc, op0=ALU.mult, op1=ALU.mult),
                               reads=[('xt', tt), ('st', 'rstd3'), 'bcs'], writes=[('xt', tt)])
                          p.dma('act', y[b, T * 128:(T + 1) * 128, :], xt[tt][:], reads=[('xt', tt)], writes=['y'])
                      p.barrier()
        except _Stop:
            pass
        p.finish()
    return nc


def prep_inputs(inputs, n_cores):
    f = lambda a: np.ascontiguousarray(np.asarray(a, dtype=np.float32))
    colv = lambda v, n: f(np.asarray(v).reshape(n, 128).T)
    cols = np.zeros((128, NCOLS), np.float32)
    cols[:, C_GATTN:C_GATTN + 8] = colv(inputs["attn_norm_g"][0], 8)
    cols[:, C_BGATE:C_BGATE + 16] = colv(inputs["b_gate"][0], 16)
    cols[:, C_DWB:C_DWB + 8] = colv(inputs["dw_b"][0], 8)
    cols[:, C_LNG:C_LNG + 8] = colv(inputs["conv_ln_g"][0], 8)
    cols[:, C_LNB:C_LNB + 8] = colv(inputs["conv_ln_b"][0], 8)
    cols[:, C_QG:C_QG + 2] = colv(inputs["q_norm_g"][0], 2)
    cols[0:64, C_KIG] = np.asarray(inputs["kidx_ln_g"][0])
    cols[0:64, C_KIB] = np.asarray(inputs["kidx_ln_b"][0])
    cols[:, C_GMLP:C_GMLP + 8] = colv(inputs["mlp_norm_g"][0], 8)
    dww = np.asarray(inputs["dw_w"][0])
    cols[:, C_DWW:C_DWW + 248] = dww.reshape(31, 8, 128).transpose(2, 1, 0).reshape(128, 248)
    bcs = np.zeros((128, 1152), np.float32)
    bcs[:, 0:128] = np.asarray(inputs["kv_norm_g"][0])[None, :]
    bcs[:, 128:1152] = np.asarray(inputs["final_norm_g"])[None, :]
    wuk = np.asarray(inputs["w_uk"][0])
    wukp = np.zeros((128, 16, 128), np.float32)
    wuv = np.asarray(inputs["w_uv"][0])
    wuvp = np.zeros((128, 16, 128), np.float32)
    for h in range(16):
        o = (h % 2) * 64
        wukp[o:o + 64, h, :] = wuk[h]
        wuvp[:, h, o:o + 64] = wuv[h]
    shared = {
        "w_in": f(inputs["w_in"][0]), "w_co": f(inputs["w_conv_out"][0]), "w_uq": f(inputs["w_uq"][0]), "w_qi": f(inputs["w_qi"][0]),
        "w_ukp": f(wukp.reshape(128, 2048)), "w_uvp": f(wuvp.reshape(128, 2048)), "w_ao": f(inputs["w_attn_out"][0]), "w_o": f(inputs["w_o"][0]),
        "w_1": f(inputs["w_ff1"][0]), "w_2": f(inputs["w_ff2"][0]), "cols": cols, "bcs": bcs,
    }
    xs = np.asarray(inputs["x"], dtype=np.float32)
    B = xs.shape[0]
    nb = B // n_cores
    maps = []
    for i in range(n_cores):
        m = dict(shared)
        m["x"] = np.ascontiguousarray(xs[i * nb:(i + 1) * nb])
        maps.append(m)
    return maps, nb


def kernel(**inputs):
    n_cores = 8
    xs = inputs["x"]
    B, L, _ = xs.shape
    maps, nb = prep_inputs(inputs, n_cores)
    nc = build(nb, L, min(256, L // 4))
    res = run_bass_kernel_spmd(nc, maps, core_ids=list(range(n_cores)))
    return np.concatenate([r["y"] for r in res.results], axis=0).astype(np.float32)
```

```python
import numpy as np
import os
DBG = os.environ.get('KDBG', '')
from contextlib import ExitStack
import concourse.bass as bass
import concourse.mybir as mybir
from concourse.bass_utils import run_bass_kernel_spmd

F32 = mybir.dt.float32
BF16 = mybir.dt.bfloat16
ALU = mybir.AluOpType
AF = mybir.ActivationFunctionType
AX = mybir.AxisListType

D = 1024
DIN = 4552
DFF = 4096
EPS = 1e-6
NIT = 16
NSLOT = 3
OAC_B = 2
DEN_B = 4
ACT_FRAC = 0.42
ACT_FRAC_B0 = 0.2
NDUM_A = 0
NDUM_B = 0
ACT_COUNT = True
NEG = -1.0e30

C_GATTN, C_BGATE, C_DWB, C_LNG, C_LNB, C_QG, C_KIG, C_KIB, C_GMLP, C_DWW = 0, 8, 24, 32, 40, 48, 50, 51, 52, 60
NCOLS = 60 + 8 * 31


class Prog:
    EPOCH = 20000

    def __init__(self, nc):
        self.nc = nc
        self.E = {'pe': nc.tensor, 'act': nc.scalar, 'dve': nc.vector, 'pool': nc.gpsimd, 'sp': nc.sync}
        self.cnt = {e: 0 for e in self.E}
        self.sems = {e: [] for e in self.E}
        self.seen = {e: {} for e in self.E}
        self.last_w = {}
        self.readers = {}
        self.ndma_sems = 8
        self.dma_sems = [nc.alloc_semaphore(name=f"dma{i}") for i in range(self.ndma_sems)]
        self.dma_cnt = [0] * self.ndma_sems
        self.dma_rr = 0
        self.bar = {e: None for e in self.E}
        self.dead = False

    def _sem(self, e, idx):
        ep = (idx - 1) // self.EPOCH
        while len(self.sems[e]) <= ep:
            self.sems[e].append(self.nc.alloc_semaphore(name=f"s_{e}_{len(self.sems[e])}"))
        return self.sems[e][ep], (idx - 1) % self.EPOCH + 1

    def _wait(self, e, dep):
        f, idx = dep
        if self.seen[e].get(f, 0) >= idx:
            return
        self.seen[e][f] = idx
        if isinstance(f, tuple):
            self.E[e].wait_ge(self.dma_sems[f[1]], idx)
        else:
            s, v = self._sem(f, idx)
            self.E[e].wait_ge(s, v)

    def _deps(self, e, reads, writes):
        deps = []
        for k in reads:
            w = self.last_w.get(k)
            if w is not None:
                if w[0] == e and e == 'pe':
                    continue
                deps.append(w)
        for k in writes:
            w = self.last_w.get(k)
            if w is not None and not (w[0] == e and e == 'pe'):
                deps.append(w)
            for r in self.readers.get(k, ()):
                if not (r[0] == e and e == 'pe'):
                    deps.append(r)
        b = self.bar[e]
        if b is not None:
            deps.extend(d for d in b if d[0] != e)
            self.bar[e] = None
        return deps

    def _record(self, me, reads, writes):
        for k in reads:
            lst = self.readers.setdefault(k, [])
            lst[:] = [r for r in lst if r[0] != me[0]]
            lst.append(me)
        for k in writes:
            self.last_w[k] = me
            self.readers[k] = []

    def op(self, e, fn, reads=(), writes=()):
        if self.dead:
            return None
        for d in self._deps(e, reads, writes):
            self._wait(e, d)
        ins = fn(self.E[e])
        self.cnt[e] += 1
        idx = self.cnt[e]
        s, v = self._sem(e, idx)
        ins.then_inc(s, 1)
        self._record((e, idx), reads, writes)
        return ins

    def group(self, e, fns, reads=(), writes=()):
        if self.dead:
            return None
        for d in self._deps(e, reads, writes):
            self._wait(e, d)
        ins = None
        for fn in fns:
            ins = fn(self.E[e])
        self.cnt[e] += 1
        idx = self.cnt[e]
        s, v = self._sem(e, idx)
        ins.then_inc(s, 1)
        self._record((e, idx), reads, writes)

    def dma(self, q, out, in_, reads=(), writes=()):
        if self.dead:
            return None
        si = self.dma_rr
        self.dma_rr = (self.dma_rr + 1) % self.ndma_sems
        for d in self._deps(q, reads, writes):
            self._wait(q, d)
        if self.dma_cnt[si] > 0:
            self._wait(q, (('dma', si), self.dma_cnt[si]))
        self.E[q].dma_start(out=out, in_=in_).then_inc(self.dma_sems[si], 16)
        self.dma_cnt[si] += 16
        me = (('dma', si), self.dma_cnt[si])
        self._record(me, reads, writes)
        return me

    def barrier(self):
        pts = [(e, self.cnt[e]) for e in self.E if self.cnt[e] > 0]
        pts += [(('dma', i), self.dma_cnt[i]) for i in range(self.ndma_sems) if self.dma_cnt[i] > 0]
        for e in self.E:
            self.bar[e] = list(pts)

    def finish(self):
        self.barrier()
        for e in self.E:
            for d in self.bar[e]:
                if d[0] != e:
                    self._wait(e, d)
            self.bar[e] = None


class _Stop(Exception):
    pass


def build(NB, L, TOPK, limit=99):
    NT = L // 128
    NCH = L // 512
    nc = bass.Bass("TRN2", target_bir_lowering=False)
    dt_in = lambda name, shape: nc.dram_tensor(name, shape, F32, kind="ExternalInput").ap()
    x = dt_in("x", [NB, L, D])
    w_in = dt_in("w_in", [D, DIN])
    w_co = dt_in("w_co", [D, D])
    w_uq = dt_in("w_uq", [256, 1024])
    w_qi = dt_in("w_qi", [256, 512])
    w_ukp = dt_in("w_ukp", [128, 16 * 128])
    w_uvp = dt_in("w_uvp", [128, 16 * 128])
    w_ao = dt_in("w_ao", [D, D])
    w_o = dt_in("w_o", [D, D])
    w_1 = dt_in("w_1", [D, DFF])
    w_2 = dt_in("w_2", [DFF, D])
    cols_d = dt_in("cols", [128, NCOLS])
    bcs_d = dt_in("bcs", [128, 128 + 1024])
    y = nc.dram_tensor("y", [NB, L, D], F32, kind="ExternalOutput").ap()
    scr = lambda name, shape: nc.dram_tensor(name, shape, BF16, kind="Internal").ap()
    s_win = scr("s_win", [D, DIN])
    s_wco = scr("s_wco", [D, D])
    s_wao = scr("s_wao", [D, D])
    s_wo = scr("s_wo", [D, D])
    s_w1 = scr("s_w1", [D, DFF])
    s_w2 = scr("s_w2", [DFF, D])
    s_dg = scr("s_dg", [8, 128, 31 * 128])

    p = Prog(nc)
    with ExitStack() as es:
        uid = [0]

        def _alloc(stack, name, shape, dt):
            uid[0] += 1
            return stack.enter_context(nc.sbuf_tensor(f"sb{uid[0]}_{name}", shape, dt))

        S = lambda name, shape, dt: _alloc(es, name, shape, dt)
        ps = es.enter_context(nc.psum_tensor("ps", [128, 8, 512], F32))

        def psT(b):
            return ps[:, b, :].bitcast(BF16)

        cols = S("cols", [128, NCOLS], F32)
        bcs = S("bcs", [128, 1152], F32)
        ident = S("ident", [128, 128], BF16)
        ones = S("ones", [128, 128], BF16)
        pw2 = S("pw2", [128, NIT + 1], F32)
        wsm = S("wsm", [128, 8, 456], BF16)
        wuq = S("wuq", [128, 2, 1024], BF16)
        wqi = S("wqi", [128, 2, 512], BF16)
        wukp = S("wukp", [128, 16, 128], BF16)
        wuvp = S("wuvp", [128, 16, 128], BF16)
        slots = [S(f"slot{i}", [128, 4096], BF16) for i in range(NSLOT)]
        ckv_tok = S("ckv_tok", [128, NT, 128], BF16)
        ckvT = S("ckvT", [128, L], BF16)
        kidxT = S("kidxT", [64, L], BF16)
        widx = S("widx", [128, NT, 8], F32)
        xt = [S(f"xt{i}", [128, D], F32) for i in range(4)]
        hT = S("hT", [128, 8, 512], BF16)
        cqT = S("cqT", [128, 2, 512], BF16)
        mergedT = S("mergedT", [128, 8, 512], BF16)
        oT = S("oT", [128, 8, 512], BF16)
        st = S("st", [128, 64], F32)
        junkAs = [S(f"junkA{i}", [128, 1024], BF16) for i in range(1)]
        jrr = [0]

        def rsqrt(out_ap, in_ap, reads, writes):
            p.op('act', lambda e: e.activation(out=in_ap, in_=in_ap, func=AF.Sqrt), reads=reads, writes=reads)
            p.op('dve', lambda e: e.reciprocal(out=out_ap, in_=in_ap), reads=reads, writes=writes)

        def sq_accum(in_ap, n, acc_ap, reads, writes):
            jt, jk = JA()
            p.op('act', lambda e: e.activation(out=jt[:, 0:n], in_=in_ap, func=AF.Square, accum_out=acc_ap), reads=reads, writes=list(writes) + [jk])

        def JA():
            return junkAs[0], ('junkA', 0)
        vT = S("vT", [128, 8, 544], BF16)

        kvg_bc = bcs[:, 0:128]
        gfin_bc = bcs[:, 128:1152]

        psr = [0]

        def psalloc(n=1):
            b = psr[0]
            if b % n:
                b += n - b % n
            if b + n > 8:
                b = 0
            psr[0] = (b + n) % 8
            return b

        def PK(b, n=1):
            return [('ps', b + i) for i in range(n)]

        slot_rr = [0]

        def wload(dram_ap, view_fn):
            s = slot_rr[0]
            slot_rr[0] = (s + 1) % NSLOT
            v = view_fn(slots[s])
            p.dma('sp', v, dram_ap, writes=[('slot', s)])
            return v, ('slot', s)

        v_k512 = lambda t: t[:, :].rearrange("p (k c) -> p k c", k=8)
        v_dg = lambda t: t[:, 0:31 * 128].rearrange("p (k c) -> p k c", k=31)

        def wcols(s_ap, c0, cw=512):
            return s_ap[:, c0:c0 + cw].rearrange("(k p) c -> p k c", p=128)

        evac_rr = [0]

        def evac_eng():
            evac_rr[0] ^= 1
            return 'act' if evac_rr[0] else 'dve'

        def evac(out, in_, reads, writes):
            e = 'dve'
            p.op(e, copy_op(e, out, in_), reads=reads, writes=writes)

        def copy_op(e, out, in_):
            if e == 'act':
                return lambda en: en.activation(out=out, in_=in_, func=AF.Copy)
            return lambda en: en.tensor_copy(out=out, in_=in_)

        p.dma('sp', cols[:], cols_d, writes=['cols'])
        p.dma('sp', bcs[:], bcs_d, writes=['bcs'])
        p.op('dve', lambda e: e.memset(ones[:], 1.0), writes=['ones'])
        for n in range(NIT + 1):
            p.op('dve', lambda e, n=n: e.memset(pw2[:, n:n + 1], 0.5 ** (n + 1)), writes=['pw2'])
        p.op('pool', lambda e: e.affine_select(out=ident[:], in_=ones[:], pattern=[[1, 128]], compare_op=ALU.is_equal,
                                               fill=0.0, base=0, channel_multiplier=-1), reads=['ones'], writes=['ident'])
        with ExitStack() as es2:
            S2 = lambda name, shape, dt: _alloc(es2, name, shape, dt)
            st32 = [S2(f"st32_{i}", [128, 2048], F32) for i in range(4)]
            st16 = [S2(f"st16_{i}", [128, 2048], BF16) for i in range(4)]
            dgs = [S2(f"dgs{i}", [128, 31, 128], BF16) for i in range(2)]
            ci = [0]

            def conv_piece(src_ap, cw, dst_dram=None, dst_sb=None, scale=None):
                s = ci[0] % 4
                ci[0] += 1
                p.dma('sp', st32[s][:, :cw], src_ap, writes=[('st32', s)])
                if dst_sb is not None:
                    o = dst_sb
                    wr = ['wres']
                else:
                    o = st16[s][:, :cw]
                    wr = [('st16', s)]
                sc = 1.0 if scale is None else scale
                p.op('dve', lambda e: e.tensor_scalar(out=o, in0=st32[s][:, :cw], scalar1=sc, scalar2=None, op0=ALU.mult),
                     reads=[('st32', s), 'cols'], writes=wr)
                if dst_dram is not None:
                    p.dma('act', dst_dram, st16[s][:, :cw], reads=[('st16', s)], writes=['scratch'])

            def convert(W, dst, R, C, scale_off=None):
                for rt in range(R // 128):
                    sc = None if scale_off is None else cols[:, scale_off + rt:scale_off + rt + 1]
                    for c0 in range(0, C, 2048):
                        cw = min(2048, C - c0)
                        conv_piece(W[rt * 128:(rt + 1) * 128, c0:c0 + cw], cw,
                                   dst_dram=dst[rt * 128:(rt + 1) * 128, c0:c0 + cw], scale=sc)

            convert(w_in, s_win, D, DIN, C_GATTN)
            convert(w_co, s_wco, D, D)
            convert(w_ao, s_wao, D, D)
            convert(w_o, s_wo, D, D)
            convert(w_1, s_w1, D, DFF, C_GMLP)
            convert(w_2, s_w2, DFF, D)
            for k in range(8):
                conv_piece(w_in[k * 128:(k + 1) * 128, 2048:2504], 456, dst_sb=wsm[:, k, :], scale=cols[:, C_GATTN + k:C_GATTN + k + 1])
            for kk in range(2):
                conv_piece(w_uq[kk * 128:(kk + 1) * 128, :], 1024, dst_sb=wuq[:, kk, :], scale=cols[:, C_QG + kk:C_QG + kk + 1])
                conv_piece(w_qi[kk * 128:(kk + 1) * 128, :], 512, dst_sb=wqi[:, kk, :], scale=cols[:, C_QG + kk:C_QG + kk + 1])
            conv_piece(w_ukp, 2048, dst_sb=wukp[:, :, :].rearrange("p a b -> p (a b)"))
            conv_piece(w_uvp, 2048, dst_sb=wuvp[:, :, :].rearrange("p a b -> p (a b)"))
            for ct in range(8):
                s = ct % 2
                fns = []
                for k in range(31):
                    cidx = C_DWW + ct * 31 + k
                    fns.append(lambda e, k=k, cidx=cidx, s=s: e.tensor_scalar(out=dgs[s][:, k, :], in0=ident[:], scalar1=cols[:, cidx:cidx + 1],
                                                                              scalar2=None, op0=ALU.mult))
                p.group('dve', fns, reads=['ident', 'cols'], writes=[('dgs', s)])
                p.dma('act', s_dg[ct], dgs[s][:, :, :].rearrange("p a b -> p (a b)"), reads=[('dgs', s)], writes=['scratch'])
            p.barrier()
        p.barrier()

        try:
          if limit < 1:
              p.dead = True
          for b in range(NB):
              for c in range(NCH):
                  t0 = c * 512
                  with ExitStack() as esa:
                      SA = lambda name, shape, dt: _alloc(esa, name, shape, dt)
                      qT = SA("qT", [128, 8, 128], BF16)
                      qlatT2 = [SA(f"qlatT{i}", [128, 16, 128], BF16) for i in range(2)]
                      qidxT = SA("qidxT", [64, 8, 128], BF16)
                      score = SA("score", [128, L], F32)
                      rtmp = [SA(f"rtmp{i}", [128, L], F32) for i in range(2)]
                      mask = SA("mask", [128, max(L, 2048)], BF16)
                      MK = [('mask', 0), ('mask', 1)]
                      junkS = mask
                      maskT2 = [SA(f"maskT{i}", [128, NT, 128], BF16) for i in range(2)]
                      Eb = [SA(f"Eb{i}", [128, 8, 128], BF16) for i in range(2)]
                      oTn = SA("oTn", [128, 8, 128], BF16)
                      bsq = SA("bsq", [128, 8], F32)
                      wh = SA("wh", [128, NIT + 1], F32)

                      def stage_Q(qi):
                          i = c * 4 + qi
                          nj = i + 1
                          Li = nj * 128
                          qs = slice(qi * 128, (qi + 1) * 128)
                          qlatT = qlatT2[qi % 2]
                          par = qi % 2
                          for mg in range(2):
                              bk = psalloc(1)
                              fns = []
                              for m in range(4):
                                  for kk in range(2):
                                      fns.append(lambda e, m=m, kk=kk, mg=mg, bk=bk: e.matmul(ps[:, bk, m * 128:(m + 1) * 128], lhsT=wuq[:, kk, (mg * 4 + m) * 128:(mg * 4 + m + 1) * 128],
                                                                                          rhs=cqT[:, kk, qs], start=(kk == 0), stop=(kk == 1)))
                              p.group('pe', fns, reads=[('cqT', qi), 'wres'], writes=PK(bk))
                              p.op('act', lambda e, mg=mg, bk=bk: e.activation(out=qT[:, mg * 4:(mg + 1) * 4, :].rearrange("p m t -> p (m t)"), in_=ps[:, bk, :], func=AF.Copy),
                                   reads=PK(bk), writes=[('qT', mg)])
                          for hg in range(4):
                              bk = psalloc(1)
                              fns = [lambda e, h=h, hg=hg, bk=bk: e.matmul(ps[:, bk, h * 128:(h + 1) * 128], lhsT=wukp[:, hg * 4 + h, :], rhs=qT[:, (hg * 4 + h) // 2, :],
                                                                       start=True, stop=True) for h in range(4)]
                              p.group('pe', fns, reads=[('qT', 0), ('qT', 1), 'wres'], writes=PK(bk))
                              p.op('act', lambda e, hg=hg, bk=bk: e.activation(out=qlatT[:, hg * 4:(hg + 1) * 4, :].rearrange("p m t -> p (m t)"), in_=ps[:, bk, :], func=AF.Copy),
                                   reads=PK(bk), writes=[('qlatT', par, hg)])
                          for hg in range(2):
                              bk = psalloc(1)
                              fns = []
                              for h in range(4):
                                  for kk in range(2):
                                      fns.append(lambda e, h=h, kk=kk, hg=hg, bk=bk: e.matmul(ps[0:64, bk, h * 128:(h + 1) * 128], lhsT=wqi[:, kk, (hg * 4 + h) * 64:(hg * 4 + h + 1) * 64],
                                                                                          rhs=cqT[:, kk, qs], start=(kk == 0), stop=(kk == 1)))
                              p.group('pe', fns, reads=[('cqT', qi), 'wres'], writes=PK(bk))
                              p.op('act', lambda e, hg=hg, bk=bk: e.activation(out=qidxT[0:64, hg * 4:(hg + 1) * 4, :].rearrange("p m t -> p (m t)"), in_=ps[0:64, bk, :], func=AF.Copy),
                                   reads=PK(bk), writes=[('qidxT', hg)])
                          nsc = (Li + 511) // 512
                          nal = 1 if nsc == 1 else (2 if nsc == 2 else 4)
                          kread = [('kidxT', T) for T in range(nj)]
                          for h in range(8):
                              b0 = psalloc(nal)
                              fns = []
                              for sc in range(nsc):
                                  w = min(512, Li - sc * 512)
                                  fns.append(lambda e, sc=sc, w=w, b0=b0, h=h: e.matmul(ps[:, b0 + sc, 0:w], lhsT=qidxT[0:64, h, :], rhs=kidxT[0:64, sc * 512:sc * 512 + w],
                                                                                     start=True, stop=True))
                              p.group('pe', fns, reads=[('qidxT', 0), ('qidxT', 1)] + kread, writes=PK(b0, nal))
                              s = h % 2
                              psv = ps[:, b0:b0 + nsc, :].rearrange("p a b -> p (a b)")[:, 0:Li]
                              p.op('act', lambda e, s=s, psv=psv: e.activation(out=rtmp[s][:, 0:Li], in_=psv, func=AF.Relu), reads=PK(b0, nal), writes=[('rtmp', s)])
                              if h == 0:
                                  p.op('dve', lambda e, s=s, i=i: e.tensor_scalar(out=score[:, 0:Li], in0=rtmp[s][:, 0:Li], scalar1=widx[:, i, 0:1], scalar2=None, op0=ALU.mult),
                                       reads=[('rtmp', s), ('widx', i)], writes=['score'])
                              else:
                                  p.op('dve', lambda e, s=s, i=i, h=h: e.scalar_tensor_tensor(out=score[:, 0:Li], in0=rtmp[s][:, 0:Li], scalar=widx[:, i, h:h + 1], in1=score[:, 0:Li],
                                                                                            op0=ALU.mult, op1=ALU.add),
                                       reads=[('rtmp', s), ('widx', i), 'score'], writes=['score'])
                          p.op('dve', lambda e: e.tensor_reduce(out=bsq[:, 0:1], in_=score[:, 0:Li], axis=AX.X, op=ALU.min), reads=['score'], writes=[('b', 'lo')])
                          p.op('dve', lambda e: e.memset(score[0:64, Li - 64:Li], NEG), reads=[('b', 'lo')], writes=['score'])
                          p.op('dve', lambda e: e.tensor_reduce(out=bsq[:, 5:6], in_=score[:, 0:Li], axis=AX.X, op=ALU.max), reads=['score'], writes=[('b', 'hi')])
                          p.op('dve', lambda e: e.tensor_tensor(out=bsq[:, 1:2], in0=bsq[:, 5:6], in1=bsq[:, 0:1], op=ALU.subtract),
                               reads=[('b', 'hi'), ('b', 'lo')], writes=[('b', 'w0')])
                          p.op('dve', lambda e: e.scalar_tensor_tensor(out=bsq[:, 0:1], in0=bsq[:, 1:2], scalar=-(2.0 ** -10), in1=bsq[:, 0:1], op0=ALU.mult, op1=ALU.add),
                               reads=[('b', 'w0'), ('b', 'lo')], writes=[('b', 'lo')])
                          p.op('dve', lambda e: e.tensor_scalar(out=bsq[:, 0:1], in0=bsq[:, 0:1], scalar1=-1e-6, scalar2=None, op0=ALU.add),
                               reads=[('b', 'lo')], writes=[('b', 'lo')])
                          p.op('dve', lambda e: e.tensor_tensor(out=bsq[:, 1:2], in0=bsq[:, 5:6], in1=bsq[:, 0:1], op=ALU.subtract),
                               reads=[('b', 'hi'), ('b', 'lo')], writes=[('b', 'w0')])
                          p.op('dve', lambda e: e.tensor_scalar(out=wh[:, :], in0=pw2[:, :], scalar1=bsq[:, 1:2], scalar2=None, op0=ALU.mult),
                               reads=[('b', 'w0'), 'pw2'], writes=['wh'])
                          p.op('dve', lambda e: e.tensor_tensor(out=bsq[:, 7:8], in0=bsq[:, 0:1], in1=wh[:, 0:1], op=ALU.add),
                               reads=[('b', 'lo'), 'wh'], writes=[('b', 'mid')])

                      def stage_B(qi):
                          i = c * 4 + qi
                          nj = i + 1
                          Li = nj * 128
                          maskT = maskT2[qi % 2]
                          par = qi % 2
                          frac = ACT_FRAC_B0 if qi == 0 else ACT_FRAC
                          H = max(64, min(Li - 64, ((int(Li * frac) + 63) // 64) * 64))
                          thr2 = float(2 * TOPK - H) - 0.5
                          if Li <= TOPK:
                              yield
                              return
                          for n in range(NIT + 1):
                              if n > 0:
                                  m = n - 1
                                  p.op('dve', lambda e: e.scalar_tensor_tensor(out=bsq[:, 5:6], in0=bsq[:, 4:5], scalar=2.0, in1=bsq[:, 3:4], op0=ALU.mult, op1=ALU.subtract),
                                       reads=[('b', 'c1'), ('b', 'c2')], writes=[('b', 'tot')])
                                  p.op('dve', lambda e, m=m: e.scalar_tensor_tensor(out=bsq[:, 6:7], in0=bsq[:, 5:6], scalar=thr2, in1=wh[:, m:m + 1], op0=ALU.is_ge, op1=ALU.mult),
                                       reads=[('b', 'tot'), 'wh'], writes=[('b', 'tmp2')])
                                  p.op('dve', lambda e, m=m: e.scalar_tensor_tensor(out=bsq[:, 7:8], in0=bsq[:, 6:7], scalar=bsq[:, 7:8], in1=wh[:, m + 1:m + 2], op0=ALU.add, op1=ALU.subtract),
                                       reads=[('b', 'tmp2'), ('b', 'mid'), 'wh'], writes=[('b', 'mid')])
                              yield
                              if n < NIT:
                                  p.op('act', lambda e: e.activation(out=mask[:, 0:H], in_=score[:, 0:H], func=AF.Sign, bias=bsq[:, 7:8], scale=-1.0, accum_out=bsq[:, 3:4]),
                                       reads=['score', ('b', 'mid')], writes=[('b', 'c1'), ('mask', 0)])
                                  p.op('dve', lambda e: e.tensor_scalar(out=mask[:, H:Li], in0=score[:, H:Li], scalar1=bsq[:, 7:8], scalar2=0.0, op0=ALU.is_ge, op1=ALU.add,
                                                                        accum_out=bsq[:, 4:5]),
                                       reads=['score', ('b', 'mid')], writes=[('b', 'c2'), ('mask', 1)])
                              yield
                          p.op('dve', lambda e: e.tensor_tensor(out=bsq[:, 0:1], in0=bsq[:, 7:8], in1=wh[:, NIT:NIT + 1], op=ALU.subtract),
                               reads=[('b', 'mid'), 'wh'], writes=[('b', 'lo')])
                          yield

                      def stage_Bfin(qi):
                          i = c * 4 + qi
                          nj = i + 1
                          Li = nj * 128
                          maskT = maskT2[qi % 2]
                          par = qi % 2
                          p.op('dve', lambda e: e.tensor_scalar(out=mask[:, 0:Li], in0=score[:, 0:Li], scalar1=bsq[:, 0:1], scalar2=None, op0=ALU.is_ge),
                               reads=['score', ('b', 'lo')], writes=MK)
                          for j0 in range(0, nj, 8):
                              n8 = min(8, nj - j0)
                              bt = psalloc(1)
                              while bt >= 4:
                                  bt = psalloc(1)
                              pt = psT(bt)
                              p.group('pe', [lambda e, jj=jj, j0=j0, pt=pt: e.transpose(out=pt[:, jj * 128:(jj + 1) * 128], in_=mask[:, (j0 + jj) * 128:(j0 + jj + 1) * 128], identity=ident[:])
                                             for jj in range(n8)], reads=MK + ['ident'], writes=PK(bt))
                              p.op('dve', lambda e, j0=j0, n8=n8, pt=pt: e.tensor_scalar(out=maskT[:, j0:j0 + n8, :].rearrange("p a b -> p (a b)"), in0=pt[:, 0:n8 * 128],
                                                                                        scalar1=-1.0, scalar2=30000.0, op0=ALU.add, op1=ALU.mult),
                                   reads=PK(bt), writes=[('maskT', par)])

                      def stage_T(qi):
                          i = c * 4 + qi
                          nj = i + 1
                          qs = slice(qi * 128, (qi + 1) * 128)
                          qlatT = qlatT2[qi % 2]
                          maskT = maskT2[qi % 2]
                          par = qi % 2
                          for hf in range(2):
                              def qk_step(j, hf=hf):
                                  sb = 0 if j % 2 == 0 else 6
                                  fns = []
                                  for g in range(2):
                                      fns.append(lambda e, g=g, j=j, sb=sb, hf=hf: e.matmul(ps[:, sb + g, :], lhsT=ckvT[:, j * 128:(j + 1) * 128],
                                                                                         rhs=qlatT[:, hf * 8 + g * 4:hf * 8 + g * 4 + 4, :], start=True, stop=False))
                                      fns.append(lambda e, g=g, j=j, sb=sb: e.matmul(ps[:, sb + g, :], lhsT=ident[:],
                                                                                  rhs=maskT[:, j, :].unsqueeze(1).broadcast_to([128, 4, 128]), start=False, stop=True))
                                  p.group('pe', fns, reads=[('ckvT', j), ('maskT', par), 'ident'] + [('qlatT', par, hg) for hg in range(4)], writes=PK(sb, 2))

                              def exp_step(j, hf=hf):
                                  sb = 0 if j % 2 == 0 else 6
                                  s = j % 2
                                  p.op('act', lambda e, sb=sb, s=s: e.activation(out=Eb[s][:, :, :].rearrange("p a b -> p (a b)"), in_=ps[:, sb:sb + 2, :].rearrange("p a b -> p (a b)"),
                                                                              func=AF.Exp, scale=0.125),
                                       reads=PK(sb, 2), writes=[('Eb', s)])

                              def rest_step(j, hf=hf):
                                  s = j % 2
                                  fns = []
                                  for g in range(2):
                                      fns.append(lambda e, g=g, j=j, s=s: e.matmul(ps[:, OAC_B + g, :], lhsT=ckv_tok[:, j, :], rhs=Eb[s][:, g * 4:(g + 1) * 4, :], start=(j == 0), stop=(j == nj - 1)))
                                      fns.append(lambda e, g=g, j=j, s=s: e.matmul(ps[:, DEN_B + g, :], lhsT=ones[:], rhs=Eb[s][:, g * 4:(g + 1) * 4, :], start=(j == 0), stop=(j == nj - 1)))
                                  p.group('pe', fns, reads=[('ckv_tok', j), ('Eb', s), 'ones'], writes=PK(OAC_B, 4))

                              qk_step(0)
                              for j in range(nj):
                                  exp_step(j)
                                  if j + 1 < nj:
                                      qk_step(j + 1)
                                  yield
                                  rest_step(j)
                                  yield
                              p.op('act', lambda e: e.activation(out=rtmp[1][:, 0:1024], in_=ps[:, DEN_B:DEN_B + 2, :].rearrange("p a b -> p (a b)"), func=AF.Ln), reads=PK(DEN_B, 2), writes=[('rtmp', 1)])
                              p.op('act', lambda e: e.activation(out=rtmp[1][:, 0:1024], in_=rtmp[1][:, 0:1024], func=AF.Exp, scale=-1.0), reads=[('rtmp', 1)], writes=[('rtmp', 1)])
                              p.op('dve', lambda e: e.tensor_tensor(out=oTn[:, :, :].rearrange("p a b -> p (a b)"), in0=ps[:, OAC_B:OAC_B + 2, :].rearrange("p a b -> p (a b)"), in1=rtmp[1][:, 0:1024], op=ALU.mult),
                                   reads=PK(OAC_B, 2) + [('rtmp', 1)], writes=['oTn'])
                              bk = 7
                              fns = []
                              for pr in range(4):
                                  h0 = hf * 8 + 2 * pr
                                  fns.append(lambda e, pr=pr, h0=h0, bk=bk: e.matmul(ps[:, bk, pr * 128:(pr + 1) * 128], lhsT=wuvp[:, h0, :], rhs=oTn[:, 2 * pr, :], start=True, stop=False))
                                  fns.append(lambda e, pr=pr, h0=h0, bk=bk: e.matmul(ps[:, bk, pr * 128:(pr + 1) * 128], lhsT=wuvp[:, h0 + 1, :], rhs=oTn[:, 2 * pr + 1, :], start=False, stop=True))
                              p.group('pe', fns, reads=['oTn', 'wres'], writes=PK(bk))
                              p.op('dve', lambda e, bk=bk, hf=hf: e.tensor_copy(out=oT[:, hf * 4:(hf + 1) * 4, qs], in_=ps[:, bk, :].rearrange("p (m t) -> p m t", m=4)),
                                   reads=PK(bk), writes=[('oT', qi)])
                              yield
                              yield

                      def interleave(ga, gb):
                          la = list_len[0]
                          alive_a, alive_b = ga is not None, gb is not None
                          while alive_a or alive_b:
                              if alive_b:
                                  try:
                                      next(gb)
                                  except StopIteration:
                                      alive_b = False
                              if alive_a:
                                  try:
                                      next(ga)
                                  except StopIteration:
                                      alive_a = False

                      list_len = [0]

                      def dbg(tag):
                          if DBG and DBG == f"{c},{tag}":
                              p.dead = True

                      with ExitStack() as esf:
                          SF = lambda name, shape, dt: _alloc(esf, name, shape, dt)
                          xs = [mask[:, i * 1024:(i + 1) * 1024] for i in range(2)]
                          yT = SF("yT", [128, 8, 512], BF16)
                          y2T = SF("y2T", [128, 8, 512], BF16)
                          sig = [SF(f"sig{i}", [128, 512], F32) for i in range(2)]
                          mean = SF("mean", [128, 512], F32)
                          rstd = SF("rstd", [128, 512], F32)
                          tmpa = [SF(f"tmpa{i}", [128, 512], F32) for i in range(2)]
                          cqn = SF("cqn", [128, 4, 256], BF16)
                          kn = SF("kn", [128, 4, 64], BF16)
                          st2 = SF("st2", [128, 64], F32)
                          smp = rtmp[1][:, 0:456]
                          if c == 0:
                              p.op('dve', lambda e: e.memset(vT[:, :, 0:30], 0.0), writes=['vT'])
                          for tt in range(4):
                              T = c * 4 + tt
                              p.dma('sp', xt[tt][:], x[b, T * 128:(T + 1) * 128, :], writes=[('xt', tt)])
                              sq_accum(xt[tt][:], 1024, st[:, tt:tt + 1], [('xt', tt)], [('st', 'ssq', tt)])
                          p.op('dve', lambda e: e.tensor_scalar(out=st[:, 4:8], in0=st[:, 0:4], scalar1=1.0 / D, scalar2=EPS, op0=ALU.mult, op1=ALU.add),
                               reads=[('st', 'ssq', i) for i in range(4)], writes=[('st', 'ms')])
                          rsqrt(st[:, 8:12], st[:, 4:8], [('st', 'ms')], [('st', 'rstd')])
                          if limit < 1.07:
                              p.dead = True
                          for tt in range(4):
                              s = tt % 2
                              p.op('dve', lambda e, tt=tt, s=s: e.tensor_scalar(out=xs[s], in0=xt[tt][:], scalar1=st[:, 8 + tt:9 + tt], scalar2=None, op0=ALU.mult),
                                   reads=[('xt', tt), ('st', 'rstd')], writes=[('mask', s)])
                              if limit < 1.08:
                                  p.dead = True
                              bk = psalloc(1)
                              pt = psT(bk)
                              p.group('pe', [lambda e, k=k, s=s, pt=pt: e.transpose(out=pt[:, k * 128:(k + 1) * 128], in_=xs[s][:, k * 128:(k + 1) * 128], identity=ident[:])
                                             for k in range(8)], reads=[('mask', s), 'ident'], writes=PK(bk))
                              if limit < 1.09:
                                  p.dead = True
                              p.op('dve', lambda e, tt=tt, pt=pt: e.tensor_copy(out=hT[:, :, tt * 128:(tt + 1) * 128], in_=pt.rearrange("p (k t) -> p k t", k=8)),
                                   reads=PK(bk), writes=[('hT', tt)])
                          hTk = [('hT', i) for i in range(4)]
                          if limit < 1.2:
                              p.dead = True
                          smps = [rtmp[tt // 2][:, (tt % 2) * 456:(tt % 2) * 456 + 456] for tt in range(4)]
                          smk = [('rtmp', tt // 2) for tt in range(4)]
                          for tt in range(4):
                              bk = psalloc(1)
                              p.group('pe', [lambda e, k=k, tt=tt, bk=bk: e.matmul(ps[:, bk, 0:456], lhsT=hT[:, k, tt * 128:(tt + 1) * 128], rhs=wsm[:, k, :],
                                                                                start=(k == 0), stop=(k == 7)) for k in range(8)],
                                      reads=[('hT', tt), 'wres'], writes=PK(bk))
                              p.op('dve', lambda e, bk=bk, tt=tt: e.tensor_copy(out=smps[tt], in_=ps[:, bk, 0:456]), reads=PK(bk), writes=[smk[tt]])
                              sq_accum(smps[tt][:, 0:256], 256, st2[:, 0 + tt:1 + tt], [smk[tt]], [('st2', 'q1', tt)])
                              sq_accum(smps[tt][:, 256:384], 128, st2[:, 4 + tt:5 + tt], [smk[tt]], [('st2', 'q2', tt)])
                              sq_accum(smps[tt][:, 384:448], 64, st2[:, 8 + tt:9 + tt], [smk[tt]], [('st2', 'q3', tt)])
                              p.op('dve', lambda e, tt=tt: e.tensor_reduce(out=st2[:, 12 + tt:13 + tt], in_=smps[tt][:, 384:448], axis=AX.X, op=ALU.add),
                                   reads=[smk[tt]], writes=[('st2', 'q4', tt)])
                          q1k = [('st2', 'q1', t) for t in range(4)]
                          q2k = [('st2', 'q2', t) for t in range(4)]
                          q3k = [('st2', 'q3', t) for t in range(4)]
                          q4k = [('st2', 'q4', t) for t in range(4)]
                          p.op('dve', lambda e: e.tensor_scalar(out=st2[:, 16:20], in0=st2[:, 0:4], scalar1=1.0 / 256, scalar2=EPS, op0=ALU.mult, op1=ALU.add),
                               reads=q1k, writes=[('st2', 'm1')])
                          p.op('dve', lambda e: e.tensor_scalar(out=st2[:, 20:24], in0=st2[:, 4:8], scalar1=1.0 / 128, scalar2=EPS, op0=ALU.mult, op1=ALU.add),
                               reads=q2k, writes=[('st2', 'm2')])
                          p.op('dve', lambda e: e.tensor_scalar(out=st2[:, 28:32], in0=st2[:, 12:16], scalar1=1.0 / 64, scalar2=None, op0=ALU.mult),
                               reads=q4k, writes=[('st2', 'mk')])
                          p.op('dve', lambda e: e.tensor_tensor(out=st2[:, 32:36], in0=st2[:, 28:32], in1=st2[:, 28:32], op=ALU.mult),
                               reads=[('st2', 'mk')], writes=[('st2', 'mk2')])
                          p.op('dve', lambda e: e.tensor_scalar(out=st2[:, 36:40], in0=st2[:, 8:12], scalar1=1.0 / 64, scalar2=EPS, op0=ALU.mult, op1=ALU.add),
                               reads=q3k, writes=[('st2', 'm3a')])
                          p.op('dve', lambda e: e.tensor_tensor(out=st2[:, 24:28], in0=st2[:, 36:40], in1=st2[:, 32:36], op=ALU.subtract),
                               reads=[('st2', 'm3a'), ('st2', 'mk2')], writes=[('st2', 'm3')])
                          rsqrt(st2[:, 40:52], st2[:, 16:28], [('st2', 'm1'), ('st2', 'm2'), ('st2', 'm3')], [('st2', 'r3')])
                          for tt in range(4):
                              T = c * 4 + tt
                              p.op('dve', lambda e, tt=tt: e.tensor_scalar(out=cqn[:, tt, :], in0=smps[tt][:, 0:256], scalar1=st2[:, 40 + tt:41 + tt], scalar2=None, op0=ALU.mult),
                                   reads=[smk[tt], ('st2', 'r3')], writes=[('cqn', tt)])
                              p.op('dve', lambda e, tt=tt, T=T: e.scalar_tensor_tensor(out=ckv_tok[:, T, :], in0=smps[tt][:, 256:384], scalar=st2[:, 44 + tt:45 + tt], in1=kvg_bc,
                                                                                      op0=ALU.mult, op1=ALU.mult),
                                   reads=[smk[tt], ('st2', 'r3'), 'bcs'], writes=[('ckv_tok', T)])
                              p.op('dve', lambda e, tt=tt: e.tensor_scalar(out=kn[:, tt, :], in0=smps[tt][:, 384:448], scalar1=st2[:, 28 + tt:29 + tt], scalar2=st2[:, 48 + tt:49 + tt],
                                                                           op0=ALU.subtract, op1=ALU.mult),
                                   reads=[smk[tt], ('st2', 'r3'), ('st2', 'mk')], writes=[('kn', tt)])
                              p.op('dve', lambda e, tt=tt, T=T: e.tensor_scalar(out=widx[:, T, :], in0=smps[tt][:, 448:456], scalar1=float((8 * 64) ** -0.5), scalar2=None, op0=ALU.mult),
                                   reads=[smk[tt]], writes=[('widx', T)])
                              bt = psalloc(1)
                              pt = psT(bt)
                              p.group('pe', [
                                  lambda e, pt=pt, tt=tt: e.transpose(out=pt[:, 0:128], in_=cqn[:, tt, 0:128], identity=ident[:]),
                                  lambda e, pt=pt, tt=tt: e.transpose(out=pt[:, 128:256], in_=cqn[:, tt, 128:256], identity=ident[:]),
                                  lambda e, pt=pt, T=T: e.transpose(out=pt[:, 256:384], in_=ckv_tok[:, T, :], identity=ident[:]),
                                  lambda e, pt=pt, tt=tt: e.transpose(out=pt[0:64, 384:512], in_=kn[:, tt, 0:64], identity=ident[:]),
                              ], reads=[('cqn', tt), ('ckv_tok', T), ('kn', tt), 'ident'], writes=PK(bt))
                              p.op('dve', lambda e, pt=pt, tt=tt: e.tensor_copy(out=cqT[:, :, tt * 128:(tt + 1) * 128], in_=pt[:, 0:256].rearrange("p (k t) -> p k t", k=2)),
                                   reads=PK(bt), writes=[('cqT', tt)])
                              p.op('dve', lambda e, pt=pt, T=T: e.tensor_copy(out=ckvT[:, T * 128:(T + 1) * 128], in_=pt[:, 256:384]),
                                   reads=PK(bt), writes=[('ckvT', T)])
                              p.op('dve', lambda e, pt=pt, T=T: e.tensor_scalar(out=kidxT[0:64, T * 128:(T + 1) * 128], in0=pt[0:64, 384:512],
                                                                               scalar1=cols[0:64, C_KIG:C_KIG + 1], scalar2=cols[0:64, C_KIB:C_KIB + 1], op0=ALU.mult, op1=ALU.add),
                                   reads=PK(bt) + ['cols'], writes=[('kidxT', T)])
                          if limit < 1.3:
                              p.dead = True
                          stage_Q(0)
                          B0 = stage_B(0)

                          def tick():
                              try:
                                  next(B0)
                              except StopIteration:
                                  pass

                          for half in range(2):
                              Wa, ka = wload(wcols(s_win, half * 512), v_k512)
                              Wg, kg = wload(wcols(s_win, 1024 + half * 512), v_k512)
                              for ctl in range(4):
                                  ct = half * 4 + ctl
                                  ba = psalloc(1)
                                  p.group('pe', [lambda e, k=k, ctl=ctl, ba=ba, Wa=Wa: e.matmul(ps[:, ba, :], lhsT=Wa[:, k, ctl * 128:(ctl + 1) * 128], rhs=hT[:, k, :],
                                                                                            start=(k == 0), stop=(k == 7)) for k in range(8)],
                                          reads=hTk + [ka], writes=PK(ba))
                                  bg = psalloc(1)
                                  p.group('pe', [lambda e, k=k, ctl=ctl, bg=bg, Wg=Wg: e.matmul(ps[:, bg, :], lhsT=Wg[:, k, ctl * 128:(ctl + 1) * 128], rhs=hT[:, k, :],
                                                                                            start=(k == 0), stop=(k == 7)) for k in range(8)],
                                          reads=hTk + [kg], writes=PK(bg))
                                  s = ct % 2
                                  p.op('act', lambda e, bg=bg, s=s: e.activation(out=sig[s][:], in_=ps[:, bg, :], func=AF.Sigmoid), reads=PK(bg), writes=[('sig', s)])
                                  p.op('dve', lambda e, ba=ba, s=s, ct=ct: e.tensor_tensor(out=vT[:, ct, 30:542], in0=ps[:, ba, :], in1=sig[s][:], op=ALU.mult),
                                       reads=PK(ba) + [('sig', s)], writes=[('vT', ct)])
                                  tick()
                          if limit < 1.4:
                              p.dead = True
                          for ct in range(8):
                              Dg, kd = wload(s_dg[ct], v_dg)
                              by = psalloc(1)
                              p.group('pe', [lambda e, k=k, ct=ct, by=by, Dg=Dg: e.matmul(ps[:, by, :], lhsT=Dg[:, k, :], rhs=vT[:, ct, k:k + 512],
                                                                                        start=(k == 0), stop=(k == 30)) for k in range(31)],
                                      reads=[('vT', ct), 'vT', kd], writes=PK(by))
                              p.op('act', lambda e, by=by, ct=ct: e.activation(out=yT[:, ct, :], in_=ps[:, by, :], func=AF.Identity, bias=cols[:, C_DWB + ct:C_DWB + ct + 1]),
                                   reads=PK(by) + ['cols'], writes=[('yT', ct)])
                              p.op('act', lambda e, by=by, ct=ct: e.activation(out=y2T[:, ct, :], in_=ps[:, by, :], func=AF.Square, bias=cols[:, C_DWB + ct:C_DWB + ct + 1]),
                                   reads=PK(by) + ['cols'], writes=[('y2T', ct)])
                              tick()
                          if limit < 1.5:
                              p.dead = True
                          if c + 1 < NCH:
                              p.op('dve', lambda e: e.tensor_copy(out=vT[:, :, 0:30], in_=vT[:, :, 512:542]),
                                   reads=[('vT', ct) for ct in range(8)], writes=['vT'])
                          bs = psalloc(1)
                          p.group('pe', [lambda e, ct=ct, bs=bs: e.matmul(ps[:, bs, :], lhsT=ones[:], rhs=yT[:, ct, :], start=(ct == 0), stop=(ct == 7)) for ct in range(8)],
                                  reads=[('yT', ct) for ct in range(8)] + ['ones'], writes=PK(bs))
                          bq = psalloc(1)
                          p.group('pe', [lambda e, ct=ct, bq=bq: e.matmul(ps[:, bq, :], lhsT=ones[:], rhs=y2T[:, ct, :], start=(ct == 0), stop=(ct == 7)) for ct in range(8)],
                                  reads=[('y2T', ct) for ct in range(8)] + ['ones'], writes=PK(bq))
                          p.op('act', lambda e, bs=bs: e.activation(out=mean[:], in_=ps[:, bs, :], func=AF.Identity, scale=1.0 / D), reads=PK(bs), writes=['mean'])
                          p.op('dve', lambda e: e.tensor_tensor(out=tmpa[0][:], in0=mean[:], in1=mean[:], op=ALU.mult), reads=['mean'], writes=[('tmpa', 0)])
                          p.op('dve', lambda e, bq=bq: e.scalar_tensor_tensor(out=tmpa[1][:], in0=ps[:, bq, :], scalar=1.0 / D, in1=tmpa[0][:], op0=ALU.mult, op1=ALU.subtract),
                               reads=PK(bq) + [('tmpa', 0)], writes=[('tmpa', 1)])
                          p.op('dve', lambda e: e.tensor_scalar(out=tmpa[0][:], in0=tmpa[1][:], scalar1=EPS, scalar2=None, op0=ALU.add),
                               reads=[('tmpa', 1)], writes=[('tmpa', 0)])
                          rsqrt(rstd[:], tmpa[0][:], [('tmpa', 0)], ['rstd'])
                          for ct in range(8):
                              s = ct % 2
                              p.op('dve', lambda e, ct=ct, s=s: e.tensor_tensor(out=tmpa[s][:], in0=yT[:, ct, :], in1=mean[:], op=ALU.subtract),
                                   reads=[('yT', ct), 'mean'], writes=[('tmpa', s)])
                              p.op('dve', lambda e, s=s: e.tensor_tensor(out=sig[s][:], in0=tmpa[s][:], in1=rstd[:], op=ALU.mult),
                                   reads=[('tmpa', s), 'rstd'], writes=[('sig', s)])
                              p.op('act', lambda e, ct=ct, s=s: e.activation(out=yT[:, ct, :], in_=sig[s][:], func=AF.Silu, scale=cols[:, C_LNG + ct:C_LNG + ct + 1],
                                                                          bias=cols[:, C_LNB + ct:C_LNB + ct + 1]),
                                   reads=[('sig', s), 'cols'], writes=[('yT', ct)])
                              tick()
                          sTk = [('yT', ct) for ct in range(8)]
                          if limit < 1.6:
                              p.dead = True
                          for half in range(2):
                              Wp, kp = wload(wcols(s_wco, half * 512), v_k512)
                              Wg, kg = wload(wcols(s_win, 2504 + half * 512), v_k512)
                              for dl in range(4):
                                  dt = half * 4 + dl
                                  ba = psalloc(1)
                                  p.group('pe', [lambda e, k=k, dl=dl, ba=ba, Wp=Wp: e.matmul(ps[:, ba, :], lhsT=Wp[:, k, dl * 128:(dl + 1) * 128], rhs=yT[:, k, :],
                                                                                           start=(k == 0), stop=(k == 7)) for k in range(8)],
                                          reads=sTk + [kp], writes=PK(ba))
                                  bg = psalloc(1)
                                  p.group('pe', [lambda e, k=k, dl=dl, bg=bg, Wg=Wg: e.matmul(ps[:, bg, :], lhsT=Wg[:, k, dl * 128:(dl + 1) * 128], rhs=hT[:, k, :],
                                                                                           start=(k == 0), stop=(k == 7)) for k in range(8)],
                                          reads=hTk + [kg], writes=PK(bg))
                                  s = dt % 2
                                  p.op('act', lambda e, bg=bg, s=s, dt=dt: e.activation(out=tmpa[s][:], in_=ps[:, bg, :], func=AF.Sigmoid, bias=cols[:, C_BGATE + dt:C_BGATE + dt + 1]),
                                       reads=PK(bg) + ['cols'], writes=[('tmpa', s)])
                                  p.op('dve', lambda e, ba=ba, s=s, dt=dt: e.tensor_tensor(out=mergedT[:, dt, :], in0=ps[:, ba, :], in1=tmpa[s][:], op=ALU.mult),
                                       reads=PK(ba) + [('tmpa', s)], writes=[('mT', dt)])
                                  tick()
                          for _ in B0:
                              pass
                      if limit < 2:
                          p.dead = True
                      stage_Bfin(0)
                      dbg("B0")
                      for qi in range(4):
                          if qi + 1 < 4:
                              stage_Q(qi + 1)
                          dbg(f"Q{qi + 1}")
                          interleave(stage_T(qi), stage_B(qi + 1) if qi + 1 < 4 else None)
                          if qi + 1 < 4:
                              stage_Bfin(qi + 1)
                          dbg(f"T{qi}")
                      p.barrier()
                  if limit < 3:
                      p.dead = True
                  with ExitStack() as esb:
                      SB_ = lambda name, shape, dt: _alloc(esb, name, shape, dt)
                      h1T = SB_("h1T", [128, 32, 512], BF16)
                      tmpb = [SB_(f"tmpb{i}", [128, 512], F32) for i in range(2)]
                      xs2 = [SB_(f"xs2_{i}", [128, D], BF16) for i in range(2)]
                      oTk = [('oT', qi) for qi in range(4)]
                      hTk = [('hT', i) for i in range(4)]
                      for half in range(2):
                          Wp, kp = wload(wcols(s_wao, half * 512), v_k512)
                          Wg, kg = wload(wcols(s_win, 3528 + half * 512), v_k512)
                          for dl in range(4):
                              dt = half * 4 + dl
                              ba = psalloc(1)
                              p.group('pe', [lambda e, k=k, dl=dl, ba=ba, Wp=Wp: e.matmul(ps[:, ba, :], lhsT=Wp[:, k, dl * 128:(dl + 1) * 128], rhs=oT[:, k, :],
                                                                                       start=(k == 0), stop=(k == 7)) for k in range(8)],
                                      reads=oTk + [kp], writes=PK(ba))
                              bg = psalloc(1)
                              p.group('pe', [lambda e, k=k, dl=dl, bg=bg, Wg=Wg: e.matmul(ps[:, bg, :], lhsT=Wg[:, k, dl * 128:(dl + 1) * 128], rhs=hT[:, k, :],
                                                                                       start=(k == 0), stop=(k == 7)) for k in range(8)],
                                      reads=hTk + [kg], writes=PK(bg))
                              s = dt % 2
                              p.op('act', lambda e, bg=bg, s=s, dt=dt: e.activation(out=tmpb[s][:], in_=ps[:, bg, :], func=AF.Sigmoid, bias=cols[:, C_BGATE + 8 + dt:C_BGATE + 9 + dt]),
                                   reads=PK(bg) + ['cols'], writes=[('tmpb', s)])
                              p.op('dve', lambda e, ba=ba, s=s: e.tensor_tensor(out=tmpb[s][:], in0=ps[:, ba, :], in1=tmpb[s][:], op=ALU.mult),
                                   reads=PK(ba) + [('tmpb', s)], writes=[('tmpb', s)])
                              p.op('dve', lambda e, s=s, dt=dt: e.tensor_tensor(out=mergedT[:, dt, :], in0=mergedT[:, dt, :], in1=tmpb[s][:], op=ALU.add),
                                   reads=[('tmpb', s), ('mT', dt)], writes=[('mT', dt)])
                      mTk = [('mT', dt) for dt in range(8)]
                      Wo = [wload(wcols(s_wo, dh * 512), v_k512) for dh in range(2)]
                      for tt in range(4):
                          for dh in range(2):
                              bk = psalloc(1)
                              W_, kw = Wo[dh]
                              p.group('pe', [lambda e, k=k, tt=tt, bk=bk, W_=W_: e.matmul(ps[:, bk, :], lhsT=mergedT[:, k, tt * 128:(tt + 1) * 128], rhs=W_[:, k, :],
                                                                                       start=(k == 0), stop=(k == 7)) for k in range(8)],
                                      reads=mTk + [kw], writes=PK(bk))
                              p.op('dve', lambda e, tt=tt, dh=dh, bk=bk: e.tensor_tensor(out=xt[tt][:, dh * 512:(dh + 1) * 512], in0=ps[:, bk, :], in1=xt[tt][:, dh * 512:(dh + 1) * 512], op=ALU.add),
                                   reads=PK(bk) + [('xt', tt)], writes=[('xt', tt)])
                          sq_accum(xt[tt][:], 1024, st[:, 32 + tt:33 + tt], [('xt', tt)], [('st', 'ssq2', tt)])
                      p.op('dve', lambda e: e.tensor_scalar(out=st[:, 36:40], in0=st[:, 32:36], scalar1=1.0 / D, scalar2=EPS, op0=ALU.mult, op1=ALU.add),
                           reads=[('st', 'ssq2', i) for i in range(4)], writes=[('st', 'ms2')])
                      rsqrt(st[:, 40:44], st[:, 36:40], [('st', 'ms2')], [('st', 'rstd2')])
                      for tt in range(4):
                          s = tt % 2
                          p.op('dve', lambda e, tt=tt, s=s: e.tensor_scalar(out=xs2[s][:], in0=xt[tt][:], scalar1=st[:, 40 + tt:41 + tt], scalar2=None, op0=ALU.mult),
                               reads=[('xt', tt), ('st', 'rstd2')], writes=[('xs2', s)])
                          bk = psalloc(1)
                          pt = psT(bk)
                          p.group('pe', [lambda e, k=k, s=s, pt=pt: e.transpose(out=pt[:, k * 128:(k + 1) * 128], in_=xs2[s][:, k * 128:(k + 1) * 128], identity=ident[:])
                                         for k in range(8)], reads=[('xs2', s), 'ident'], writes=PK(bk))
                          p.op('dve', lambda e, tt=tt, pt=pt: e.tensor_copy(out=hT[:, :, tt * 128:(tt + 1) * 128], in_=pt.rearrange("p (k t) -> p k t", k=8)),
                               reads=PK(bk), writes=[('hT', tt)])
                      for fc in range(8):
                          W1, k1 = wload(wcols(s_w1, fc * 512), v_k512)
                          for fl in range(4):
                              f = fc * 4 + fl
                              bk = psalloc(1)
                              p.group('pe', [lambda e, k=k, fl=fl, bk=bk, W1=W1: e.matmul(ps[:, bk, :], lhsT=W1[:, k, fl * 128:(fl + 1) * 128], rhs=hT[:, k, :],
                                                                                       start=(k == 0), stop=(k == 7)) for k in range(8)],
                                      reads=hTk + [k1], writes=PK(bk))
                              s = f % 2
                              p.op('act', lambda e, bk=bk, s=s: e.activation(out=tmpb[s][:], in_=ps[:, bk, :], func=AF.Relu), reads=PK(bk), writes=[('tmpb', s)])
                              p.op('dve', lambda e, s=s, f=f: e.tensor_tensor(out=h1T[:, f, :], in0=tmpb[s][:], in1=tmpb[s][:], op=ALU.mult),
                                   reads=[('tmpb', s)], writes=[('h1T', f)])
                      for dh in range(2):
                          for fc in range(4):
                              W2, k2 = wload(s_w2[fc * 1024:(fc + 1) * 1024, dh * 512:(dh + 1) * 512].rearrange("(k p) c -> p k c", p=128), v_k512)
                              fns = []
                              for fl in range(8):
                                  f = fc * 8 + fl
                                  for tt in range(4):
                                      fns.append(lambda e, f=f, fl=fl, tt=tt, W2=W2: e.matmul(ps[:, tt, :], lhsT=h1T[:, f, tt * 128:(tt + 1) * 128], rhs=W2[:, fl, :],
                                                                                           start=(f == 0), stop=(f == 31)))
                              p.group('pe', fns, reads=[('h1T', fc * 8 + fl) for fl in range(8)] + [k2], writes=PK(0, 4))
                          for tt in range(4):
                              p.op('dve', lambda e, tt=tt, dh=dh: e.tensor_tensor(out=xt[tt][:, dh * 512:(dh + 1) * 512], in0=ps[:, tt, :], in1=xt[tt][:, dh * 512:(dh + 1) * 512], op=ALU.add),
                                   reads=PK(tt) + [('xt', tt)], writes=[('xt', tt)])
                      psr[0] = 4
                      for tt in range(4):
                          sq_accum(xt[tt][:], 1024, st[:, 44 + tt:45 + tt], [('xt', tt)], [('st', 'ssq3', tt)])
                      p.op('dve', lambda e: e.tensor_scalar(out=st[:, 48:52], in0=st[:, 44:48], scalar1=1.0 / D, scalar2=EPS, op0=ALU.mult, op1=ALU.add),
                           reads=[('st', 'ssq3', i) for i in range(4)], writes=[('st', 'ms3')])
                      rsqrt(st[:, 52:56], st[:, 48:52], [('st', 'ms3')], [('st', 'rstd3')])
                      for tt in range(4):
                          T = c * 4 + tt
                          p.op('dve', lambda e, tt=tt: e.scalar_tensor_tensor(out=xt[tt][:], in0=xt[tt][:], scalar=st[:, 52 + tt:53 + tt], in1=gfin_bc, op0=ALU.mult, op1=ALU.mult),
                               reads=[('xt', tt), ('st', 'rstd3'), 'bcs'], writes=[('xt', tt)])
                          p.dma('act', y[b, T * 128:(T + 1) * 128, :], xt[tt][:], reads=[('xt', tt)], writes=['y'])
                      p.barrier()
        except _Stop:
            pass
        p.finish()
    return nc


def prep_inputs(inputs, n_cores):
    f = lambda a: np.ascontiguousarray(np.asarray(a, dtype=np.float32))
    colv = lambda v, n: f(np.asarray(v).reshape(n, 128).T)
    cols = np.zeros((128, NCOLS), np.float32)
    cols[:, C_GATTN:C_GATTN + 8] = colv(inputs["attn_norm_g"][0], 8)
    cols[:, C_BGATE:C_BGATE + 16] = colv(inputs["b_gate"][0], 16)
    cols[:, C_DWB:C_DWB + 8] = colv(inputs["dw_b"][0], 8)
    cols[:, C_LNG:C_LNG + 8] = colv(inputs["conv_ln_g"][0], 8)
    cols[:, C_LNB:C_LNB + 8] = colv(inputs["conv_ln_b"][0], 8)
    cols[:, C_QG:C_QG + 2] = colv(inputs["q_norm_g"][0], 2)
    cols[0:64, C_KIG] = np.asarray(inputs["kidx_ln_g"][0])
    cols[0:64, C_KIB] = np.asarray(inputs["kidx_ln_b"][0])
    cols[:, C_GMLP:C_GMLP + 8] = colv(inputs["mlp_norm_g"][0], 8)
    dww = np.asarray(inputs["dw_w"][0])
    cols[:, C_DWW:C_DWW + 248] = dww.reshape(31, 8, 128).transpose(2, 1, 0).reshape(128, 248)
    bcs = np.zeros((128, 1152), np.float32)
    bcs[:, 0:128] = np.asarray(inputs["kv_norm_g"][0])[None, :]
    bcs[:, 128:1152] = np.asarray(inputs["final_norm_g"])[None, :]
    wuk = np.asarray(inputs["w_uk"][0])
    wukp = np.zeros((128, 16, 128), np.float32)
    wuv = np.asarray(inputs["w_uv"][0])
    wuvp = np.zeros((128, 16, 128), np.float32)
    for h in range(16):
        o = (h % 2) * 64
        wukp[o:o + 64, h, :] = wuk[h]
        wuvp[:, h, o:o + 64] = wuv[h]
    shared = {
        "w_in": f(inputs["w_in"][0]), "w_co": f(inputs["w_conv_out"][0]), "w_uq": f(inputs["w_uq"][0]), "w_qi": f(inputs["w_qi"][0]),
        "w_ukp": f(wukp.reshape(128, 2048)), "w_uvp": f(wuvp.reshape(128, 2048)), "w_ao": f(inputs["w_attn_out"][0]), "w_o": f(inputs["w_o"][0]),
        "w_1": f(inputs["w_ff1"][0]), "w_2": f(inputs["w_ff2"][0]), "cols": cols, "bcs": bcs,
    }
    xs = np.asarray(inputs["x"], dtype=np.float32)
    B = xs.shape[0]
    nb = B // n_cores
    maps = []
    for i in range(n_cores):
        m = dict(shared)
        m["x"] = np.ascontiguousarray(xs[i * nb:(i + 1) * nb])
        maps.append(m)
    return maps, nb


def kernel(**inputs):
    n_cores = 8
    xs = inputs["x"]
    B, L, _ = xs.shape
    maps, nb = prep_inputs(inputs, n_cores)
    nc = build(nb, L, min(256, L // 4))
    res = run_bass_kernel_spmd(nc, maps, core_ids=list(range(n_cores)))
    return np.concatenate([r["y"] for r in res.results], axis=0).astype(np.float32)
```
